# Optimizing a Trainium2 kernel written in Bass

```python
import functools
import jax, jax.numpy as jnp
from jax import lax
import numpy as np

D_MODEL = 1024
BATCH = 8
SEQ = 2048
DEPTH = 1
DEC_BATCH = 128
DEC_SEQ = 4
PAST_LEN = 8192
PAGE_SIZE = 128

RET_HEADS = 4
RET_DK = 128
RET_DV = 256
RET_CHUNK = 128
MLA_HEADS = 8
MLA_Q_LORA = 384
MLA_KV_LORA = 256
MLA_D_NOPE = 128
MLA_D_ROPE = 64
MLA_D_V = 128
MLA_Q_BLOCK = 128
N_MEM = 256
X_HEADS = 4
X_HD = 64
N_BRANCH = 3
D_FF = -(-8 * D_MODEL // (3 * 256)) * 256
ROPE_BASE = 10000.0
RMS_EPS = 1e-6
IN_SIZES = (RET_HEADS * RET_DK, RET_HEADS * RET_DK, RET_HEADS * RET_DV, RET_HEADS * RET_DV,
            MLA_Q_LORA, MLA_KV_LORA, MLA_D_ROPE, X_HEADS * X_HD, N_BRANCH * D_MODEL)
D_IN = (2 * RET_HEADS * RET_DK + 2 * RET_HEADS * RET_DV + MLA_Q_LORA + MLA_KV_LORA
        + MLA_D_ROPE + X_HEADS * X_HD + N_BRANCH * D_MODEL)

kernel_name = 'hybrid_retention_mla_memory_decoder_step'


def rms_norm(x, g):
    xf = x.astype(jnp.float32)
    y = xf * lax.rsqrt(jnp.mean(xf * xf, axis=-1, keepdims=True) + RMS_EPS)
    return (y * g.astype(jnp.float32)).astype(x.dtype)


def rope(x, pos):
    half = x.shape[-1] // 2
    inv = ROPE_BASE ** (-jnp.arange(half, dtype=jnp.float32) / half)
    ang = pos.astype(jnp.float32)[:, None] * inv[None, :]
    cos, sin = jnp.cos(ang)[:, None, :], jnp.sin(ang)[:, None, :]
    xf = x.astype(jnp.float32)
    x1, x2 = xf[..., :half], xf[..., half:]
    return jnp.concatenate([x1 * cos - x2 * sin, x1 * sin + x2 * cos], axis=-1).astype(x.dtype)


def split_cols(z):
    out, off = [], 0
    for n in IN_SIZES:
        out.append(z[..., off:off + n])
        off += n
    return out


def ret_log_gamma():
    return jnp.log1p(-jnp.exp2(-5.0 - jnp.arange(RET_HEADS, dtype=jnp.float32)))


def retention_chunk(q, k, v, s):
    L = q.shape[2]
    lg = ret_log_gamma()
    i = jnp.arange(L, dtype=jnp.float32)
    diff = i[:, None] - i[None, :]
    decay = jnp.where(diff[None] >= 0, jnp.exp(jnp.maximum(diff, 0.0)[None] * lg[:, None, None]), 0.0)
    q_dec = jnp.exp((i[None, :] + 1.0) * lg[:, None])
    k_dec = jnp.exp((L - 1.0 - i[None, :]) * lg[:, None])
    inner = jnp.einsum('bhid,bhjd->bhij', q, k) * decay
    o = (jnp.einsum('bhij,bhje->bhie', inner, v)
         + jnp.einsum('bhid,bhde->bhie', q * q_dec[None, :, :, None], s))
    s_new = (s * jnp.exp(L * lg)[None, :, None, None]
             + jnp.einsum('bhjd,bhje->bhde', k * k_dec[None, :, :, None], v))
    return o, s_new


def retention_prompt(q, k, v):
    b, S = q.shape[:2]
    nc = S // RET_CHUNK

    def to_chunks(t):
        return t.reshape(b, nc, RET_CHUNK, RET_HEADS, t.shape[-1]).transpose(1, 0, 3, 2, 4)

    def step(s, xs):
        qc, kc, vc = xs
        o, s = retention_chunk(qc, kc, vc, s)
        return s, o

    s0 = jnp.zeros((b, RET_HEADS, RET_DK, RET_DV), jnp.float32)
    s_fin, o = lax.scan(step, s0, (to_chunks(q), to_chunks(k), to_chunks(v)))
    o = o.transpose(1, 0, 3, 2, 4).reshape(b, S, RET_HEADS, RET_DV)
    return o, s_fin


def retention_sample(q, k, v, s0):
    tr = lambda t: t.transpose(0, 2, 1, 3)
    o, s = retention_chunk(tr(q), tr(k), tr(v), s0)
    return tr(o), s


def mla_prompt_attend(q_lat, q_pe, ckv, kpe):
    b, S = q_lat.shape[:2]
    nb = S // MLA_Q_BLOCK
    scale = (MLA_D_NOPE + MLA_D_ROPE) ** -0.5
    kpos = jnp.arange(S)

    def blk(xs):
        ql, qp, start = xs
        s = jnp.einsum('bqhl,bkl->bhqk', ql, ckv) + jnp.einsum('bqhr,bkr->bhqk', qp, kpe)
        qpos = start + jnp.arange(MLA_Q_BLOCK)
        mask = kpos[None, :] <= qpos[:, None]
        s = jnp.where(mask, s.astype(jnp.float32) * scale, -jnp.inf)
        p = jax.nn.softmax(s, axis=-1).astype(ckv.dtype)
        return jnp.einsum('bhqk,bkl->bqhl', p, ckv)

    def to_blocks(t):
        return t.reshape(b, nb, MLA_Q_BLOCK, *t.shape[2:]).swapaxes(0, 1)

    o = lax.map(blk, (to_blocks(q_lat), to_blocks(q_pe), jnp.arange(nb) * MLA_Q_BLOCK))
    return o.swapaxes(0, 1).reshape(b, S, MLA_HEADS, MLA_KV_LORA)


def mla_sample_attend(q_lat, q_pe, ckv_new, kpe_new, ckv_past, kpe_past):
    T = q_lat.shape[1]
    P = ckv_past.shape[1]
    scale = (MLA_D_NOPE + MLA_D_ROPE) ** -0.5
    s_past = (jnp.einsum('bqhl,bkl->bhqk', q_lat, ckv_past)
              + jnp.einsum('bqhr,bkr->bhqk', q_pe, kpe_past)).astype(jnp.float32) * scale
    s_new = (jnp.einsum('bqhl,bkl->bhqk', q_lat, ckv_new)
             + jnp.einsum('bqhr,bkr->bhqk', q_pe, kpe_new)).astype(jnp.float32) * scale
    causal = jnp.arange(T)[None, :] <= jnp.arange(T)[:, None]
    s_new = jnp.where(causal, s_new, -jnp.inf)
    p = jax.nn.softmax(jnp.concatenate([s_past, s_new], axis=-1), axis=-1)
    p_past = p[..., :P].astype(ckv_past.dtype)
    p_new = p[..., P:].astype(ckv_new.dtype)
    return (jnp.einsum('bhqk,bkl->bqhl', p_past, ckv_past)
            + jnp.einsum('bhqk,bkl->bqhl', p_new, ckv_new))


def mem_kv(mem, p):
    b, m, _ = mem.shape
    mn = rms_norm(mem, p['norm_mem'])
    k = (mn @ p['w_mem_k']).reshape(b, m, X_HEADS, X_HD)
    v = (mn @ p['w_mem_v']).reshape(b, m, X_HEADS, X_HD)
    return k, v


def mem_attend(q, mk, mv):
    s = jnp.einsum('bshd,bmhd->bhsm', q, mk).astype(jnp.float32) * (X_HD ** -0.5)
    p = jax.nn.softmax(s, axis=-1).astype(mv.dtype)
    return jnp.einsum('bhsm,bmhd->bshd', p, mv)


def decoder_layer(x, pos, mem_k, mem_v, p, ret_fn, mla_fn):
    b, s, _ = x.shape
    u = rms_norm(x, p['norm_mix_pre'])
    rq, rk, rv, rg, cq, ckv, kpe, xq, gates = split_cols(u @ p['w_in'])

    rq = rope(rq.reshape(b, s, RET_HEADS, RET_DK), pos).astype(jnp.float32)
    rk = rope(rk.reshape(b, s, RET_HEADS, RET_DK), pos).astype(jnp.float32) * (RET_DK ** -0.5)
    rv = rv.reshape(b, s, RET_HEADS, RET_DV).astype(jnp.float32)
    o_ret, ret_state = ret_fn(rq, rk, rv)
    o_ret = o_ret * lax.rsqrt(jnp.mean(o_ret * o_ret, axis=-1, keepdims=True) + RMS_EPS)
    o_ret = (jax.nn.silu(rg.astype(jnp.float32)) * o_ret.reshape(b, s, RET_HEADS * RET_DV)).astype(x.dtype)
    a_ret = o_ret @ p['w_ret_o']

    cq = rms_norm(cq, p['norm_q_lat'])
    q = (cq @ p['w_uq']).reshape(b, s, MLA_HEADS, MLA_D_NOPE + MLA_D_ROPE)
    q_nope, q_pe = q[..., :MLA_D_NOPE], rope(q[..., MLA_D_NOPE:], pos)
    q_lat = jnp.einsum('bshd,hld->bshl', q_nope, p['w_uk'])
    ckv = rms_norm(ckv, p['norm_kv_lat'])
    kpe = rope(kpe[:, :, None, :], pos)[:, :, 0, :]
    o_lat = mla_fn(q_lat, q_pe, ckv, kpe)
    o_mla = jnp.einsum('bshl,hld->bshd', o_lat, p['w_uv']).reshape(b, s, MLA_HEADS * MLA_D_V)
    a_mla = o_mla @ p['w_mla_o']

    o_x = mem_attend(xq.reshape(b, s, X_HEADS, X_HD), mem_k, mem_v).reshape(b, s, X_HEADS * X_HD)
    a_x = o_x @ p['w_x_o']

    g = jax.nn.sigmoid(gates.astype(jnp.float32)).reshape(b, s, N_BRANCH, D_MODEL)
    mixed = (g[:, :, 0] * a_ret + g[:, :, 1] * a_mla + g[:, :, 2] * a_x).astype(x.dtype)
    h = x + rms_norm(mixed @ p['w_out'], p['norm_mix_post'])
    f = rms_norm(h, p['norm_ffn_pre'])
    f = (jax.nn.silu(f @ p['w_ffn_gate']) * (f @ p['w_ffn_up'])) @ p['w_ffn_down']
    y = h + rms_norm(f, p['norm_ffn_post'])
    return y, ret_state, ckv, kpe


def setup_inputs(seed: int = 0) -> dict:
    key = jax.random.key(seed)
    ks = jax.random.split(key, 32)
    n_pages = PAST_LEN // PAGE_SIZE
    n_used = DEC_BATCH * n_pages
    n_pool = n_used + max(1, n_used // 4)

    def nrm(i, shape, scale=1.0):
        return jax.random.normal(ks[i], shape, jnp.float32) * scale

    def gain(i, n):
        return 1.0 + nrm(i, (DEPTH, n), 0.05)

    page_table = jax.random.permutation(ks[5], n_pool)[:n_used].reshape(DEC_BATCH, n_pages).astype(jnp.int32)
    return {
        'x_prompt': nrm(0, (BATCH, SEQ, D_MODEL)),
        'x_sample': nrm(1, (DEC_BATCH, DEC_SEQ, D_MODEL)),
        'mem_prompt': nrm(2, (BATCH, N_MEM, D_MODEL)),
        'cache_ckv': nrm(3, (DEPTH, n_pool, PAGE_SIZE, MLA_KV_LORA)),
        'cache_kpe': nrm(4, (DEPTH, n_pool, PAGE_SIZE, MLA_D_ROPE)),
        'page_table': page_table,
        'state_ret': nrm(6, (DEPTH, DEC_BATCH, RET_HEADS, RET_DK, RET_DV), 0.5),
        'cache_mem_k': nrm(7, (DEPTH, DEC_BATCH, N_MEM, X_HEADS, X_HD)),
        'cache_mem_v': nrm(8, (DEPTH, DEC_BATCH, N_MEM, X_HEADS, X_HD)),
        'norm_mix_pre': gain(9, D_MODEL),
        'norm_mix_post': gain(10, D_MODEL),
        'norm_ffn_pre': gain(11, D_MODEL),
        'norm_ffn_post': gain(12, D_MODEL),
        'norm_mem': gain(13, D_MODEL),
        'norm_q_lat': gain(14, MLA_Q_LORA),
        'norm_kv_lat': gain(15, MLA_KV_LORA),
        'w_in': nrm(16, (DEPTH, D_MODEL, D_IN), D_MODEL ** -0.5),
        'w_uq': nrm(17, (DEPTH, MLA_Q_LORA, MLA_HEADS * (MLA_D_NOPE + MLA_D_ROPE)), MLA_Q_LORA ** -0.5),
        'w_uk': nrm(18, (DEPTH, MLA_HEADS, MLA_KV_LORA, MLA_D_NOPE), MLA_KV_LORA ** -0.5),
        'w_uv': nrm(19, (DEPTH, MLA_HEADS, MLA_KV_LORA, MLA_D_V), MLA_KV_LORA ** -0.5),
        'w_mem_k': nrm(20, (DEPTH, D_MODEL, X_HEADS * X_HD), D_MODEL ** -0.5),
        'w_mem_v': nrm(21, (DEPTH, D_MODEL, X_HEADS * X_HD), D_MODEL ** -0.5),
        'w_ret_o': nrm(22, (DEPTH, RET_HEADS * RET_DV, D_MODEL), (RET_HEADS * RET_DV) ** -0.5),
        'w_mla_o': nrm(23, (DEPTH, MLA_HEADS * MLA_D_V, D_MODEL), (MLA_HEADS * MLA_D_V) ** -0.5),
        'w_x_o': nrm(24, (DEPTH, X_HEADS * X_HD, D_MODEL), (X_HEADS * X_HD) ** -0.5),
        'w_out': nrm(25, (DEPTH, D_MODEL, D_MODEL), D_MODEL ** -0.5),
        'w_ffn_gate': nrm(26, (DEPTH, D_MODEL, D_FF), D_MODEL ** -0.5),
        'w_ffn_up': nrm(27, (DEPTH, D_MODEL, D_FF), D_MODEL ** -0.5),
        'w_ffn_down': nrm(28, (DEPTH, D_FF, D_MODEL), D_FF ** -0.5),
    }


def reference(x_prompt, x_sample, mem_prompt, cache_ckv, cache_kpe, page_table, state_ret,
              cache_mem_k, cache_mem_v, norm_mix_pre, norm_mix_post, norm_ffn_pre, norm_ffn_post,
              norm_mem, norm_q_lat, norm_kv_lat, w_in, w_uq, w_uk, w_uv, w_mem_k, w_mem_v,
              w_ret_o, w_mla_o, w_x_o, w_out, w_ffn_gate, w_ffn_up, w_ffn_down):
    db = x_sample.shape[0]
    past_len = page_table.shape[1] * cache_ckv.shape[2]
    pos_p = jnp.arange(x_prompt.shape[1])
    pos_s = past_len + jnp.arange(x_sample.shape[1])

    y_prompt, y_sample = x_prompt, x_sample
    ckv_p_l, kpe_p_l, ckv_s_l, kpe_s_l = [], [], [], []
    ret_p_l, ret_s_l, mk_p_l, mv_p_l = [], [], [], []
    for l in range(DEPTH):
        p = dict(norm_mix_pre=norm_mix_pre[l], norm_mix_post=norm_mix_post[l],
                 norm_ffn_pre=norm_ffn_pre[l], norm_ffn_post=norm_ffn_post[l],
                 norm_mem=norm_mem[l], norm_q_lat=norm_q_lat[l], norm_kv_lat=norm_kv_lat[l],
                 w_in=w_in[l], w_uq=w_uq[l], w_uk=w_uk[l], w_uv=w_uv[l],
                 w_mem_k=w_mem_k[l], w_mem_v=w_mem_v[l], w_ret_o=w_ret_o[l],
                 w_mla_o=w_mla_o[l], w_x_o=w_x_o[l], w_out=w_out[l],
                 w_ffn_gate=w_ffn_gate[l], w_ffn_up=w_ffn_up[l], w_ffn_down=w_ffn_down[l])

        mk_p, mv_p = mem_kv(mem_prompt, p)
        y_prompt, ret_p, ckv_p, kpe_p = decoder_layer(
            y_prompt, pos_p, mk_p, mv_p, p, retention_prompt, mla_prompt_attend)

        ckv_past = cache_ckv[l][page_table].reshape(db, past_len, MLA_KV_LORA)
        kpe_past = cache_kpe[l][page_table].reshape(db, past_len, MLA_D_ROPE)
        ret_fn = functools.partial(retention_sample, s0=state_ret[l].astype(jnp.float32))
        mla_fn = functools.partial(mla_sample_attend, ckv_past=ckv_past, kpe_past=kpe_past)
        y_sample, ret_s, ckv_s, kpe_s = decoder_layer(
            y_sample, pos_s, cache_mem_k[l], cache_mem_v[l], p, ret_fn, mla_fn)

        ckv_p_l.append(ckv_p)
        kpe_p_l.append(kpe_p)
        ckv_s_l.append(ckv_s)
        kpe_s_l.append(kpe_s)
        ret_p_l.append(ret_p.astype(x_prompt.dtype))
        ret_s_l.append(ret_s.astype(state_ret.dtype))
        mk_p_l.append(mk_p)
        mv_p_l.append(mv_p)

    new_ckv_prompt = jnp.stack(ckv_p_l)
    new_kpe_prompt = jnp.stack(kpe_p_l)
    new_ckv_sample = jnp.stack(ckv_s_l)
    new_kpe_sample = jnp.stack(kpe_s_l)
    new_ret_prompt = jnp.stack(ret_p_l)
    new_ret_sample = jnp.stack(ret_s_l)
    new_mem_k_prompt = jnp.stack(mk_p_l)
    new_mem_v_prompt = jnp.stack(mv_p_l)
    return (y_prompt, y_sample, new_ckv_prompt, new_kpe_prompt, new_ckv_sample, new_kpe_sample,
            new_ret_prompt, new_ret_sample, new_mem_k_prompt, new_mem_v_prompt)
```

```python
from contextlib import ExitStack
import numpy as np
import ml_dtypes
import concourse.bass as bass
import concourse.mybir as mybir
from concourse.bass_utils import run_bass_kernel_spmd

F32 = mybir.dt.float32
BF16 = mybir.dt.bfloat16
I32 = mybir.dt.int32
AF = mybir.ActivationFunctionType
ALU = mybir.AluOpType
AX = mybir.AxisListType

D = 1024
KC = 8
RH, DK, DV = 4, 128, 256
MH, QL, KVL, DN, DR, DVH = 8, 384, 256, 128, 64, 128
NMEM, XH, XD = 256, 4, 64
DFF = 2816
FC = DFF // 128
EPS = 1e-6
DEC_SEQ = 4
NEG = -1.0e30
SCALE = float((DN + DR) ** -0.5)
XSCALE = float(XD ** -0.5)
ROPE_BASE = 10000.0
SAME_ENGINE_SYNC = True


class Buf:
    def __init__(self, ap, name):
        self.ap = ap
        self.name = name
        self.w = None
        self.r = {}
        self.dsem = {}
        self.dcnt = {}


class Eng:
    def __init__(self, name, sem, is_pe=False):
        self.name = name
        self.sem = sem
        self.cnt = 0
        self.ops = []
        self.waited = {}
        self.is_pe = is_pe


class K:
    def __init__(self, nc, es):
        self.nc = nc
        self.es = es
        self.E = {}
        for n in ("pe", "act", "dve", "pool", "sp"):
            sem = es.enter_context(nc.semaphore("s_" + n))
            self.E[n] = Eng(n, sem, is_pe=(n == "pe"))
        self.stores = []
        self.nrec = 0
        self.cut = 0

    def _skip(self):
        self.nrec += 1
        return self.cut and self.nrec > self.cut

    def _wait(self, e, s, v):
        if e.waited.get(id(s), 0) >= v:
            return
        e.waited[id(s)] = v
        e.ops.append((lambda h, s=s, v=v: h.wait_ge(s, v), None))

    def _deps(self, e, R, W):
        deps = {}

        def add(sp):
            s, v = sp
            if id(s) not in deps or deps[id(s)][1] < v:
                deps[id(s)] = (s, v)
        for b in R:
            if b.w is not None:
                add(b.w)
        for b in W:
            if b.w is not None:
                add(b.w)
            for sp in b.r.values():
                add(sp)
        for (s, v) in deps.values():
            if s is e.sem and (e.is_pe or not SAME_ENGINE_SYNC):
                continue
            self._wait(e, s, v)

    def _done(self, sp, R, W):
        s, v = sp
        for b in R:
            if id(s) not in b.r or b.r[id(s)][1] < v:
                b.r[id(s)] = (s, v)
        for b in W:
            b.w = sp
            b.r = {}

    def op(self, en, fn, R=(), W=()):
        self.ops(en, [fn], R, W)

    def ops(self, en, fns, R=(), W=()):
        if self._skip():
            return
        e = self.E[en]
        self._deps(e, R, W)
        for f in fns[:-1]:
            e.ops.append((f, None))
        e.cnt += 1
        e.ops.append((fns[-1], (e.sem, 1)))
        self._done((e.sem, e.cnt), R, W)

    def dma(self, q, out_ap, in_ap, sbuf, load, R=(), W=(), pre=None, fn=None):
        if self._skip():
            return
        e = self.E[q]
        if q not in sbuf.dsem:
            sbuf.dsem[q] = self.es.enter_context(self.nc.semaphore("d%s_%s" % (q, sbuf.name)))
            sbuf.dcnt[q] = 0
        Rl = list(R) + ([] if load else [sbuf])
        Wl = list(W) + ([sbuf] if load else [])
        self._deps(e, Rl, Wl)
        sbuf.dcnt[q] += 16
        if pre is not None:
            e.ops.append((pre, None))
        if fn is None:
            fn = (lambda h: h.dma_start(out=out_ap, in_=in_ap))
        e.ops.append((fn, (sbuf.dsem[q], 16)))
        sp = (sbuf.dsem[q], sbuf.dcnt[q])
        self._done(sp, Rl, Wl)
        if not load:
            self.stores.append(sp)

    def _store_points(self):
        last = {}
        for (s, v) in self.stores:
            if id(s) not in last or last[id(s)][1] < v:
                last[id(s)] = (s, v)
        return list(last.values())

    def barrier(self):
        pts = [(eng.sem, eng.cnt) for eng in self.E.values() if eng.cnt] + self._store_points()
        for eng in self.E.values():
            for (s, v) in pts:
                if s is eng.sem:
                    continue
                self._wait(eng, s, v)

    def finish(self):
        e = self.E["sp"]
        for (s, v) in self._store_points():
            self._wait(e, s, v)
        for n, eng in self.E.items():
            if n != "sp" and eng.cnt:
                self._wait(e, eng.sem, eng.cnt)

    def emit(self):
        nc = self.nc
        with nc.Block() as block:
            def run(eng):
                def f(h):
                    for fn, inc in eng.ops:
                        ins = fn(h)
                        if inc is not None:
                            ins.then_inc(inc[0], inc[1])
                return f
            block.tensor(run(self.E["pe"]))
            block.scalar(run(self.E["act"]))
            block.vector(run(self.E["dve"]))
            block.gpsimd(run(self.E["pool"]))
            block.sync(run(self.E["sp"]))


def build(TP, NPG, NPOOL, NB):
    PS = NB * DEC_SEQ
    assert PS == 64
    NT = TP + 1
    TALL = TP * 128 + PS
    nc = bass.Bass("TRN2", target_bir_lowering=False)
    es = ExitStack()

    def din(name, shape, dt=F32):
        return nc.dram_tensor(name, list(shape), dt, kind="ExternalInput").ap()

    def dout(name, shape, dt=F32):
        return nc.dram_tensor(name, list(shape), dt, kind="ExternalOutput").ap()

    x_all = din("x_all", [TALL, D])
    mem_p = din("mem_p", [NMEM, D])
    c_ckv = din("c_ckv", [NPOOL, 128, KVL])
    c_kpe = din("c_kpe", [NPOOL, 128, DR])
    ptab = din("ptab", [1, NB * NPG], I32)
    st_in = din("st_in", [NB * RH * DK, DV])
    cmk = din("cmk", [NB * NMEM, XH * XD])
    cmv = din("cmv", [NB * NMEM, XH * XD])
    g_pre = din("g_pre", [1, D])
    g_post = din("g_post", [1, D])
    g_fpre = din("g_fpre", [1, D])
    g_fpost = din("g_fpost", [1, D])
    g_mem = din("g_mem", [1, D])
    g_ql = din("g_ql", [1, QL])
    g_kv = din("g_kv", [1, KVL])
    w_in = din("w_in", [D, 7104])
    w_uq = din("w_uq", [QL, MH * (DN + DR)])
    w_uk = din("w_uk", [MH * KVL, DN])
    w_uv = din("w_uv", [MH * KVL, DVH])
    w_mk = din("w_mk", [D, XH * XD])
    w_mv = din("w_mv", [D, XH * XD])
    w_ro = din("w_ro", [RH * DV, D])
    w_mo = din("w_mo", [MH * DVH, D])
    w_xo = din("w_xo", [XH * XD, D])
    w_o = din("w_o", [D, D])
    w_fg = din("w_fg", [D, DFF])
    w_fu = din("w_fu", [D, DFF])
    w_fd = din("w_fd", [DFF, D])
    c_ident = din("c_ident", [128, 128], BF16)
    c_cosR = din("c_cosR", [TALL, 512])
    c_sinR = din("c_sinR", [TALL, 512])
    c_cosM = din("c_cosM", [TALL, 512])
    c_sinM = din("c_sinM", [TALL, 512])
    c_dtp = din("c_dtp", [128, RH * 128])
    c_dts = din("c_dts", [128, RH * 128])
    c_qdp = din("c_qdp", [128, RH * 128])
    c_qds = din("c_qds", [128, RH * 128])
    c_kdp = din("c_kdp", [128, RH * DK])
    c_kds = din("c_kds", [128, RH * DK])
    c_bm = din("c_bm", [128, NB * PS], BF16)
    c_bmr = din("c_bmr", [128, NB], BF16)
    c_caus = din("c_caus", [128, 128], BF16)
    c_goff = din("c_goff", [128, NB * (NPG // 8)], I32)
    c_big = din("c_big", [128, 128])

    y_all = dout("y_all", [TALL, D])
    o_ckv = dout("o_ckv", [TALL, KVL])
    o_kpe = dout("o_kpe", [TALL, DR])
    o_retp = dout("o_retp", [RH * DK, DV])
    o_rets = dout("o_rets", [NB * RH * DK, DV])
    o_mk = dout("o_mk", [NMEM, XH * XD])
    o_mv = dout("o_mv", [NMEM, XH * XD])
    wd_scr = nc.dram_tensor("wd_scr", [DFF, D], BF16, kind="Internal").ap()

    k = K(nc, es)
    es.enter_context(nc.allow_low_precision("bf16 matmul operands, fp32 accumulation"))
    es.enter_context(nc.allow_non_contiguous_dma("tiny per-partition gain column loads"))

    def alloc(scope, name, shape, dt):
        return Buf(scope.enter_context(nc.sbuf_tensor(name, list(shape), dt)), name)

    ident = alloc(es, "ident", [128, 128], BF16)
    mix_t = es.enter_context(nc.sbuf_tensor("mix", [128, NT, D], BF16))
    mix = [Buf(mix_t[:, t, :], "mix%d" % t) for t in range(NT)]
    junk = alloc(es, "junk", [128, 1024], F32)
    gpre_bc = alloc(es, "gpre_bc", [128, D], F32)
    epsb = alloc(es, "epsb", [128, 1], F32)
    k.op("dve", lambda h: h.memset(epsb.ap[:, :], EPS), [], [epsb])

    S_t = es.enter_context(nc.psum_tensor("ps_s", [128, 2048], F32))
    S = [Buf(S_t[:, i * 512:(i + 1) * 512], "S%d" % i) for i in range(4)]
    T_t = es.enter_context(nc.psum_tensor("ps_t", [128, 2048], BF16))
    Tt = [Buf(T_t[:, i * 1024:(i + 1) * 1024], "T%d" % i) for i in range(2)]
    O_b = Buf(es.enter_context(nc.psum_tensor("ps_o", [128, 512], F32)), "O")
    G_b = Buf(es.enter_context(nc.psum_tensor("ps_g", [128, 512], F32)), "G")
    s_i = [0]
    t_i = [0]

    def nextS():
        s_i[0] = (s_i[0] + 1) % 4
        return S[s_i[0]]

    def nextT():
        t_i[0] = (t_i[0] + 1) % 2
        return Tt[t_i[0]]

    k.dma("sp", ident.ap[:, :], c_ident[:, :], ident, True)
    k.dma("sp", gpre_bc.ap[:, :], g_pre[0].partition_broadcast(128), gpre_bc, True)

    def rows(t):
        return 128 if t < TP else PS

    def tok0(t):
        return t * 128

    ev_i = [0]

    def evac(out_ap, in_ap, R, W, scale=None, eng=None):
        ev_i[0] += 1
        if eng is None:
            eng = "act" if (scale is not None or ev_i[0] % 2 == 0) else "dve"
        if eng == "act":
            if scale is None:
                k.op("act", lambda h: h.activation(out=out_ap, in_=in_ap, func=AF.Copy), R, W)
            else:
                k.op("act", lambda h: h.activation(out=out_ap, in_=in_ap, func=AF.Copy, scale=scale), R, W)
        else:
            assert scale is None
            k.op(eng, lambda h: h.tensor_copy(out=out_ap, in_=in_ap), R, W)

    def load_w(dst, dram, nrows, ncols):
        for c in range(nrows // 128):
            k.dma("pool", dst.ap[:, c, 0:ncols], dram[c * 128:(c + 1) * 128, 0:ncols], dst, True)

    def rstd_of(pieces, R, P, small, n, col=0):
        np_ = len(pieces)
        for i, ap in enumerate(pieces):
            k.op("act", lambda h, ap=ap, i=i: h.activation(
                out=junk.ap[:P, 0:ap.shape[-1]], in_=ap, func=AF.Square,
                accum_out=small.ap[:P, 10 + i:11 + i]), list(R), [junk, small])
        if np_ > 1:
            k.op("dve", lambda h: h.tensor_reduce(out=small.ap[:P, 9:10], in_=small.ap[:P, 10:10 + np_],
                                                  axis=AX.X, op=ALU.add), [small], [small])
            src = small.ap[:P, 9:10]
        else:
            src = small.ap[:P, 10:11]
        k.op("act", lambda h: h.activation(out=small.ap[:P, 8:9], in_=src, func=AF.Ln, scale=1.0 / n, bias=epsb.ap[:P, 0:1]),
             [small, epsb], [small])
        k.op("act", lambda h: h.activation(out=small.ap[:P, col:col + 1], in_=small.ap[:P, 8:9], func=AF.Exp,
                                           scale=-0.5), [small], [small])

    def transpose_blocks(src_buf, src_ap_fn, nblk, P, dst_buf, dst_ap_fn, bw=128, R=()):
        b0 = 0
        while b0 < nblk:
            nb_ = min(8, nblk - b0)
            tb = nextT()
            fns = []
            for j in range(nb_):
                fns.append(lambda h, j=j, b0=b0, tb=tb: h.transpose(
                    tb.ap[:bw, j * 128:j * 128 + P], src_ap_fn(b0 + j), ident.ap[:P, :P]))
            k.ops("pe", fns, [src_buf, ident] + list(R), [tb])
            src3 = tb.ap[:bw, 0:nb_ * 128].rearrange("p (b c) -> p b c", c=128)[:, :, 0:P]
            evac(dst_ap_fn(b0, nb_), src3, [tb], [dst_buf])
            b0 += nb_

    def dense(lhs_buf, lhs_fn, nkc, w_buf, w_fn, P, ncols, consume, R=(), hold=False):
        g = 0
        res = []
        for c0 in range(0, ncols, 512):
            w = min(512, ncols - c0)
            ps = nextS()
            fns = []
            for c in range(nkc):
                fns.append(lambda h, c=c, c0=c0, w=w, ps=ps: h.matmul(
                    ps.ap[:P, 0:w], lhs_fn(c), w_fn(c, c0, w), start=(c == 0), stop=(c == nkc - 1)))
            k.ops("pe", fns, [lhs_buf, w_buf] + list(R), [ps])
            if hold:
                res.append((g, c0, w, ps, ps.ap[:P, 0:w]))
            else:
                consume(g, c0, w, ps, ps.ap[:P, 0:w])
            g += 1
        return res

    class UMaker:
        def __init__(self, scope, tag):
            self.xt = [alloc(scope, "xt%s%d" % (tag, i), [128, D], F32) for i in range(2)]
            self.xn = alloc(scope, "xn" + tag, [128, D], BF16)
            self.uT = [alloc(scope, "uT%s%d" % (tag, i), [128, KC, 128], BF16) for i in range(2)]
            self.sm = alloc(scope, "smu" + tag, [128, 16], F32)
            self.made = {}
            self.loaded = set()

        def get(self, t):
            self.load(t)
            if t not in self.made:
                self.made[t] = self.make(t)
            return self.made[t]

        def prefetch(self, t):
            if t < NT:
                self.get(t)

        def load(self, t):
            if t >= NT or t in self.loaded:
                return
            self.loaded.add(t)
            P = rows(t)
            k.dma("sp", self.xt[t % 2].ap[:P, :], x_all[tok0(t):tok0(t) + P, :], self.xt[t % 2], True)

        def make(self, t):
            P = rows(t)
            xb, u = self.xt[t % 2], self.uT[t % 2]
            rstd_of([xb.ap[:P, :]], [xb], P, self.sm, D)
            k.op("dve", lambda h: h.scalar_tensor_tensor(
                out=self.xn.ap[:P, :], in0=xb.ap[:P, :], scalar=self.sm.ap[:P, 0:1], in1=gpre_bc.ap[:P, :],
                op0=ALU.mult, op1=ALU.mult), [xb, self.sm, gpre_bc], [self.xn])
            transpose_blocks(self.xn, lambda b: self.xn.ap[:P, b * 128:(b + 1) * 128], 8, P, u,
                             lambda b0, n: u.ap[:, b0:b0 + n, 0:P])
            return u

    def rope(src, dst, cb, sb_, P, nh, half, tmp1, tmp2):
        W_ = nh * 2 * half
        sv = src.ap[:P, 0:W_].rearrange("p (h two f) -> p h two f", two=2, f=half)
        tv = tmp2.ap[:P, 0:W_].rearrange("p (h two f) -> p h two f", two=2, f=half)
        snv = sb_.ap[:P, 0:W_].rearrange("p (h two f) -> p h two f", two=2, f=half)
        k.op("pool", lambda h: h.tensor_tensor(out=tmp1.ap[:P, 0:W_], in0=src.ap[:P, 0:W_], in1=cb.ap[:P, 0:W_],
                                               op=ALU.mult), [src, cb], [tmp1])
        k.op("dve", lambda h: h.tensor_tensor(out=tv[:, :, 0, :], in0=sv[:, :, 1, :], in1=snv[:, :, 0, :],
                                              op=ALU.mult), [src, sb_], [tmp2])
        k.op("dve", lambda h: h.tensor_tensor(out=tv[:, :, 1, :], in0=sv[:, :, 0, :], in1=snv[:, :, 1, :],
                                              op=ALU.mult), [src, sb_], [tmp2])
        k.op("dve", lambda h: h.tensor_tensor(out=dst.ap[:P, 0:W_], in0=tmp1.ap[:P, 0:W_], in1=tmp2.ap[:P, 0:W_],
                                              op=ALU.add), [tmp1, tmp2], [dst])

    def gated_add(t, P, uTb, lhs_buf, lhs_fn, nkc, Wo, Wg, sgb, first=False):
        def consG2(g, c0, w, ps, ps_ap):
            k.op("act", lambda h: h.activation(out=sgb.ap[:P, c0:c0 + w], in_=ps_ap, func=AF.Exp, scale=-1.0),
                 [ps], [sgb])
            k.op("dve", lambda h: h.tensor_scalar(out=sgb.ap[:P, c0:c0 + w], in0=sgb.ap[:P, c0:c0 + w], scalar1=1.0,
                                                  scalar2=None, op0=ALU.add), [sgb], [sgb])
            k.op("dve", lambda h: h.reciprocal(out=sgb.ap[:P, c0:c0 + w], in_=sgb.ap[:P, c0:c0 + w]), [sgb], [sgb])
        dense(uTb, lambda c: uTb.ap[:, c, 0:P], KC, Wg, lambda c, c0, w: Wg.ap[:, c, c0:c0 + w], P, D, consG2)

        def consA2(g, c0, w, ps, ps_ap):
            if first:
                k.op("dve", lambda h: h.tensor_tensor(out=mix[t].ap[:P, c0:c0 + w], in0=ps_ap,
                                                      in1=sgb.ap[:P, c0:c0 + w], op=ALU.mult), [ps, sgb], [mix[t]])
            else:
                k.op("dve", lambda h: h.tensor_tensor(out=sgb.ap[:P, c0:c0 + w], in0=ps_ap, in1=sgb.ap[:P, c0:c0 + w],
                                                      op=ALU.mult), [ps, sgb], [sgb])
                k.op("pool", lambda h: h.tensor_tensor(out=mix[t].ap[:P, c0:c0 + w], in0=mix[t].ap[:P, c0:c0 + w],
                                                       in1=sgb.ap[:P, c0:c0 + w], op=ALU.add), [sgb, mix[t]], [mix[t]])
        dense(lhs_buf, lhs_fn, nkc, Wo, lambda c, c0, w: Wo.ap[:, c, c0:c0 + w], P, D, consA2)

    gam = [1.0 - 2.0 ** (-5.0 - h) for h in range(RH)]

    esR = ExitStack()

    def sbR(name, shape, dt):
        return alloc(esR, name, shape, dt)

    WR = sbR("WR", [128, KC, 3072], BF16)
    Wro = sbR("Wro", [128, KC, D], BF16)
    WgA = sbR("WgA", [128, KC, D], BF16)
    um = UMaker(esR, "R")
    um.load(0)
    load_w(WR, w_in[:, 0:3072], D, 3072)
    load_w(Wro, w_ro, RH * DV, D)
    load_w(WgA, w_in[:, 7104 - 3072:7104 - 2048], D, D)
    dtc = sbR("dtc", [128, RH * 128], F32)
    qdc = sbR("qdc", [128, RH * 128], F32)
    kdc = sbR("kdc", [128, RH * DK], F32)
    bm = sbR("bm", [128, NB * PS], BF16)
    bmr = sbR("bmr", [128, NB], BF16)
    for (b_, d_) in ((dtc, c_dtp), (qdc, c_qdp), (kdc, c_kdp), (bm, c_bm), (bmr, c_bmr)):
        k.dma("sp", b_.ap[:, :], d_[:, :], b_, True)
    cosb = sbR("cosb", [128, 512], F32)
    sinb = sbR("sinb", [128, 512], F32)
    qf = sbR("qf", [128, 512], F32)
    kf = sbR("kf", [128, 512], F32)
    rt1 = sbR("rt1", [128, 512], F32)
    rt2 = sbR("rt2", [128, 512], F32)
    q_r = sbR("q_r", [128, 512], BF16)
    k_r = sbR("k_r", [128, 512], BF16)
    kd = sbR("kd", [128, 512], BF16)
    qd_r = sbR("qd_r", [128, 512], BF16)
    v_b = sbR("v_b", [128, 1024], BF16)
    sgr = sbR("sgr", [128, 1024], F32)
    qT = sbR("qT", [128, RH, 128], BF16)
    kT = sbR("kT", [128, RH, 128], BF16)
    qdT = sbR("qdT", [128, RH, 128], BF16)
    im = sbR("im", [128, 128], BF16)
    og = sbR("og", [128, 1024], BF16)
    ogT = sbR("ogT", [128, KC, 128], BF16)
    sga = sbR("sga", [128, 1024], F32)
    smR = sbR("smR", [128, 16], F32)
    s_f = sbR("s_f", [128, RH, DV], F32)
    s_b = sbR("s_b", [128, RH, DV], BF16)
    qdTm = sbR("qdTm", [128, NB, PS], BF16)
    kdm = sbR("kdm", [128, NB, DK], BF16)
    s0f = [sbR("s0f%d" % i, [128, DV], F32) for i in range(3)]
    s0b = [sbR("s0b%d" % i, [128, DV], BF16) for i in range(3)]
    snw = [sbR("snw%d" % i, [128, DV], F32) for i in range(2)]

    k.op("dve", lambda h: h.memset(s_f.ap[:, :, :], 0.0), [], [s_f])
    k.op("pool", lambda h: h.memset(s_b.ap[:, :, :], 0.0), [], [s_b])

    def _tileR(t, um):
        P = rows(t)
        smp = (t == TP)
        uTb = um.get(t)
        um.load(t + 1)
        if smp:
            for (b_, d_) in ((dtc, c_dts), (qdc, c_qds), (kdc, c_kds)):
                k.dma("sp", b_.ap[:, :], d_[:, :], b_, True)
        k.dma("sp", cosb.ap[:P, :], c_cosR[tok0(t):tok0(t) + P, :], cosb, True)
        k.dma("sp", sinb.ap[:P, :], c_sinR[tok0(t):tok0(t) + P, :], sinb, True)

        def consR(g, c0, w, ps, ps_ap, P=P):
            if g == 0:
                evac(qf.ap[:P, :], ps_ap, [ps], [qf])
            elif g == 1:
                evac(kf.ap[:P, :], ps_ap, [ps], [kf])
            elif g in (2, 3):
                evac(v_b.ap[:P, (g - 2) * 512:(g - 1) * 512], ps_ap, [ps], [v_b])
            else:
                k.op("act", lambda h: h.activation(out=sgr.ap[:P, (g - 4) * 512:(g - 3) * 512], in_=ps_ap,
                                                   func=AF.Silu), [ps], [sgr])

        dense(uTb, lambda c, uTb=uTb, P=P: uTb.ap[:, c, 0:P], KC, WR, lambda c, c0, w: WR.ap[:, c, c0:c0 + w],
              P, 3072, consR)
        um.prefetch(t + 1)
        rope(qf, q_r, cosb, sinb, P, RH, 64, rt1, rt2)
        rope(kf, k_r, cosb, sinb, P, RH, 64, rt1, rt2)
        k.op("pool", lambda h, P=P: h.tensor_tensor(out=kd.ap[:P, :], in0=k_r.ap[:P, :], in1=kdc.ap[:P, :],
                                                    op=ALU.mult), [k_r, kdc], [kd])
        k.op("pool", lambda h, P=P: h.tensor_tensor(out=qd_r.ap[:P, :], in0=q_r.ap[:P, :], in1=qdc.ap[:P, :],
                                                    op=ALU.mult), [q_r, qdc], [qd_r])
        tb = nextT()
        fns = []
        for j in range(4):
            fns.append(lambda h, j=j, tb=tb, P=P: h.transpose(tb.ap[:, j * 128:j * 128 + P],
                                                            q_r.ap[:P, j * 128:(j + 1) * 128], ident.ap[:P, :P]))
        for j in range(4):
            fns.append(lambda h, j=j, tb=tb, P=P: h.transpose(tb.ap[:, (4 + j) * 128:(4 + j) * 128 + P],
                                                            k_r.ap[:P, j * 128:(j + 1) * 128], ident.ap[:P, :P]))
        k.ops("pe", fns, [q_r, k_r, ident], [tb])
        tq = tb.ap[:, 0:512].rearrange("p (b c) -> p b c", c=128)[:, :, 0:P]
        tk = tb.ap[:, 512:1024].rearrange("p (b c) -> p b c", c=128)[:, :, 0:P]
        k.op("act", lambda h, tq=tq, P=P: h.activation(out=qT.ap[:, :, 0:P], in_=tq, func=AF.Copy), [tb], [qT])
        k.op("act", lambda h, tk=tk, P=P: h.activation(out=kT.ap[:, :, 0:P], in_=tk, func=AF.Copy), [tb], [kT])
        transpose_blocks(qd_r, lambda b, P=P: qd_r.ap[:P, b * 128:(b + 1) * 128], 4, P, qdT,
                         lambda b0, n, P=P: qdT.ap[:, b0:b0 + n, 0:P])
        for hd in range(RH):
            k.ops("pe", [lambda h, hd=hd, P=P: h.matmul(G_b.ap[:P, 0:P], kT.ap[:, hd, 0:P], qT.ap[:, hd, 0:P],
                                                      start=True, stop=True)], [kT, qT], [G_b])
            k.op("dve", lambda h, hd=hd, P=P: h.tensor_tensor(
                out=im.ap[:P, 0:P], in0=G_b.ap[:P, 0:P], in1=dtc.ap[:P, hd * 128:hd * 128 + P], op=ALU.mult),
                [G_b, dtc], [im])
            if not smp:
                fns = [lambda h, hd=hd, P=P: h.matmul(O_b.ap[:P, 0:DV], im.ap[:P, 0:P],
                                                    v_b.ap[:P, hd * DV:(hd + 1) * DV], start=True, stop=False),
                       lambda h, hd=hd, P=P: h.matmul(O_b.ap[:P, 0:DV], qdT.ap[:, hd, 0:P], s_b.ap[:, hd, :],
                                                    start=False, stop=True)]
                k.ops("pe", fns, [im, v_b, qdT, s_b], [O_b])
            else:
                k.op("pool", lambda h, hd=hd: h.tensor_tensor(
                    out=qdTm.ap[:, :, :],
                    in0=bm.ap[:, :].rearrange("p (b c) -> p b c", c=PS),
                    in1=qdT.ap[:, hd:hd + 1, 0:PS].to_broadcast([128, NB, PS]), op=ALU.mult), [bm, qdT], [qdTm])
                k.op("pool", lambda h, hd=hd: h.tensor_tensor(
                    out=kdm.ap[:PS, :, :],
                    in0=bmr.ap[:PS, :].unsqueeze(2).to_broadcast([PS, NB, DK]),
                    in1=kd.ap[:PS, hd * DK:(hd + 1) * DK].unsqueeze(1).to_broadcast([PS, NB, DK]), op=ALU.mult),
                    [bmr, kd], [kdm])
                k.ops("pe", [lambda h, hd=hd, P=P: h.matmul(O_b.ap[:P, 0:DV], im.ap[:P, 0:P],
                                                          v_b.ap[:P, hd * DV:(hd + 1) * DV], start=True, stop=False)],
                      [im, v_b], [O_b])
                for b in range(NB):
                    i3 = (hd * NB + b) % 3
                    r0 = (b * RH + hd) * DK
                    k.dma("sp", s0f[i3].ap[:, :], st_in[r0:r0 + DK, :], s0f[i3], True)
                    evac(s0b[i3].ap[:, :], s0f[i3].ap[:, :], [s0f[i3]], [s0b[i3]])
                    k.ops("pe", [lambda h, b=b, i3=i3, P=P: h.matmul(O_b.ap[:P, 0:DV], qdTm.ap[:, b, :],
                                                                   s0b[i3].ap[:, :], start=False,
                                                                   stop=(b == NB - 1))],
                          [qdTm, s0b[i3]], [O_b])
                    k.ops("pe", [lambda h, b=b, hd=hd: h.matmul(G_b.ap[:, 0:DV], kdm.ap[:PS, b, :],
                                                               v_b.ap[:PS, hd * DV:(hd + 1) * DV],
                                                               start=True, stop=True)], [kdm, v_b], [G_b])
                    sw = snw[b % 2]
                    k.op("dve", lambda h, hd=hd, i3=i3, sw=sw: h.scalar_tensor_tensor(
                        out=sw.ap[:, :], in0=s0f[i3].ap[:, :], scalar=float(gam[hd] ** DEC_SEQ), in1=G_b.ap[:, 0:DV],
                        op0=ALU.mult, op1=ALU.add), [s0f[i3], G_b], [sw])
                    k.dma("pool", o_rets[r0:r0 + DK, :], sw.ap[:, :], sw, False)
            rstd_of([O_b.ap[:P, 0:DV]], [O_b], P, smR, DV, col=hd)
            k.op("dve", lambda h, hd=hd, P=P: h.scalar_tensor_tensor(
                out=og.ap[:P, hd * DV:(hd + 1) * DV], in0=O_b.ap[:P, 0:DV], scalar=smR.ap[:P, hd:hd + 1],
                in1=sgr.ap[:P, hd * DV:(hd + 1) * DV], op0=ALU.mult, op1=ALU.mult), [O_b, smR, sgr], [og])
            if not smp:
                k.ops("pe", [lambda h, hd=hd, P=P: h.matmul(G_b.ap[:, 0:DV], kd.ap[:P, hd * DK:(hd + 1) * DK],
                                                          v_b.ap[:P, hd * DV:(hd + 1) * DV], start=True, stop=True)],
                      [kd, v_b], [G_b])
                k.op("dve", lambda h, hd=hd: h.scalar_tensor_tensor(
                    out=s_f.ap[:, hd, :], in0=s_f.ap[:, hd, :], scalar=float(gam[hd] ** 128), in1=G_b.ap[:, 0:DV],
                    op0=ALU.mult, op1=ALU.add), [s_f, G_b], [s_f])
                k.op("act", lambda h, hd=hd: h.activation(out=s_b.ap[:, hd, :], in_=s_f.ap[:, hd, :], func=AF.Copy),
                     [s_f], [s_b])
        if t == TP - 1:
            k.dma("pool", o_retp.rearrange("(h d) e -> d h e", d=DK), s_f.ap[:, :, :], s_f, False)
        transpose_blocks(og, lambda b, P=P: og.ap[:P, b * 128:(b + 1) * 128], 8, P, ogT,
                         lambda b0, n, P=P: ogT.ap[:, b0:b0 + n, 0:P])
        gated_add(t, P, uTb, ogT, lambda c, P=P: ogT.ap[:, c, 0:P], KC, Wro, WgA, sga, first=True)
    for t in range(NT):
        _tileR(t, um)
    k.barrier()
    esR.close()

    esM = ExitStack()

    def sbM(name, shape, dt):
        return alloc(esM, name, shape, dt)

    WM = sbM("WM", [128, KC, 704], BF16)
    Wuq = sbM("Wuq", [128, 3, 1536], BF16)
    wukT = sbM("wukT", [128, MH, KVL], BF16)
    Wuv = sbM("Wuv", [128, MH * 2, DVH], BF16)
    Wmo = sbM("Wmo", [128, MH, D], BF16)
    WgB = sbM("WgB", [128, KC, D], BF16)
    um = UMaker(esM, "M")
    um.load(0)
    load_w(WM, w_in[:, 3072:3776], D, 704)
    uqv = w_uq.rearrange("k (h c) -> k h c", c=DN + DR)
    gql_bc = sbM("gql_bc", [128, QL], F32)
    k.dma("sp", gql_bc.ap[:, :], g_ql[0].partition_broadcast(128), gql_bc, True)
    for c in range(3):
        for (lo, hi, off) in ((0, DN, 0), (DN, DN + DR, MH * DN)):
            wpart = hi - lo
            k.dma("pool", Wuq.ap[:, c, off:off + MH * wpart].rearrange("p (h c) -> p h c", c=wpart),
                  uqv[c * 128:(c + 1) * 128, :, lo:hi], Wuq, True)
    ol = sbM("ol", [128, MH * 2, 128], BF16)
    load_w(ol, w_uk, MH * KVL, DN)
    transpose_blocks(ol, lambda b: ol.ap[:, b, :], MH * 2, 128, wukT,
                     lambda b0, n: wukT.ap[:, :, :].rearrange("p h (lc c) -> p (h lc) c", c=128)[:, b0:b0 + n, :])
    load_w(Wuv, w_uv, MH * KVL, DVH)
    load_w(Wmo, w_mo, MH * DVH, D)
    load_w(WgB, w_in[:, 7104 - 2048:7104 - 1024], D, D)
    gkv = sbM("gkv", [128, KVL], F32)
    k.dma("sp", gkv.ap[:, :], g_kv[0].partition_broadcast(128), gkv, True)
    cosm = sbM("cosm", [128, 512], F32)
    sinm = sbM("sinm", [128, 512], F32)
    smM = sbM("smM", [128, 16], F32)
    cqn = sbM("cqn", [128, QL], BF16)
    cqT = sbM("cqT", [128, 3, 128], BF16)
    ckf = [sbM("ckf%d" % i, [128, KVL], F32) for i in range(2)]
    kpf = sbM("kpf", [128, DR], F32)
    kpr = [sbM("kpr%d" % i, [128, DR], F32) for i in range(2)]
    kp2 = sbM("kp2", [128, DR], BF16)
    mt1 = sbM("mt1", [128, 512], F32)
    mt2 = sbM("mt2", [128, 512], F32)
    qn_b = sbM("qn_b", [128, MH * DN], BF16)
    qpf = sbM("qpf", [128, 512], F32)
    qp_b = sbM("qp_b", [128, 512], BF16)
    qnT = sbM("qnT", [128, MH, 128], BF16)
    qpT = sbM("qpT", [64, MH, 128], BF16)
    qlT = sbM("qlT", [128, MH * 2, 128], BF16)
    olT = sbM("olT", [128, MH * 2, 128], BF16)
    omT = sbM("omT", [128, MH, 128], BF16)
    sgb = sbM("sgb", [128, D], F32)

    def mla_front(t, uTb, ckvT_ap, kpeT_ap, V_ap, Vbuf, ckvT, kpeT):
        P = rows(t)
        cb, sn = cosm, sinm
        k.dma("sp", cb.ap[:P, :], c_cosM[tok0(t):tok0(t) + P, :], cb, True)
        k.dma("sp", sn.ap[:P, :], c_sinM[tok0(t):tok0(t) + P, :], sn, True)
        ck, kr = ckf[t % 2], kpr[t % 2]

        def consQ(g, c0, w, ps, ps_ap):
            rstd_of([ps_ap], [ps], P, smM, QL, col=0)
            k.op("dve", lambda h: h.scalar_tensor_tensor(
                out=cqn.ap[:P, :], in0=ps_ap, scalar=smM.ap[:P, 0:1], in1=gql_bc.ap[:P, :],
                op0=ALU.mult, op1=ALU.mult), [ps, smM, gql_bc], [cqn])
        dense(uTb, lambda c: uTb.ap[:, c, 0:P], KC, WM, lambda c, c0, w: WM.ap[:, c, c0:c0 + w], P, QL, consQ)

        def consK(g, c0, w, ps, ps_ap):
            rstd_of([ps.ap[:P, 0:KVL]], [ps], P, smM, KVL, col=1)
            k.op("dve", lambda h: h.scalar_tensor_tensor(
                out=ck.ap[:P, :], in0=ps.ap[:P, 0:KVL], scalar=smM.ap[:P, 1:2], in1=gkv.ap[:P, :],
                op0=ALU.mult, op1=ALU.mult), [ps, smM, gkv], [ck])
            k.op("act", lambda h: h.activation(out=kpf.ap[:P, :], in_=ps.ap[:P, KVL:KVL + DR], func=AF.Copy),
                 [ps], [kpf])
        dense(uTb, lambda c: uTb.ap[:, c, 0:P], KC, WM, lambda c, c0, w: WM.ap[:, c, QL + c0:QL + c0 + w], P,
              KVL + DR, consK)
        k.dma("pool", o_ckv[tok0(t):tok0(t) + P, :], ck.ap[:P, :], ck, False)
        k.op("act", lambda h: h.activation(out=V_ap, in_=ck.ap[:P, :], func=AF.Copy), [ck], [Vbuf])
        transpose_blocks(Vbuf, lambda b: V_ap[:, b * 128:(b + 1) * 128], 2, P, ckvT, lambda b0, n: ckvT_ap[:, b0:b0 + n, :])
        kv = kpf.ap[:P, :].rearrange("p (two f) -> p two f", two=2)
        tv = mt2.ap[:P, 0:DR].rearrange("p (two f) -> p two f", two=2)
        sv = sn.ap[:P, 0:DR].rearrange("p (two f) -> p two f", two=2)
        k.op("pool", lambda h: h.tensor_tensor(out=mt1.ap[:P, 0:DR], in0=kpf.ap[:P, :], in1=cb.ap[:P, 0:DR],
                                               op=ALU.mult), [kpf, cb], [mt1])
        k.op("dve", lambda h: h.tensor_tensor(out=tv[:, 0, :], in0=kv[:, 1, :], in1=sv[:, 0, :], op=ALU.mult),
             [kpf, sn], [mt2])
        k.op("dve", lambda h: h.tensor_tensor(out=tv[:, 1, :], in0=kv[:, 0, :], in1=sv[:, 1, :], op=ALU.mult),
             [kpf, sn], [mt2])
        k.op("dve", lambda h: h.tensor_tensor(out=kr.ap[:P, :], in0=mt1.ap[:P, 0:DR], in1=mt2.ap[:P, 0:DR],
                                              op=ALU.add), [mt1, mt2], [kr])
        k.dma("pool", o_kpe[tok0(t):tok0(t) + P, :], kr.ap[:P, :], kr, False)
        k.op("act", lambda h: h.activation(out=kp2.ap[:P, 0:DR], in_=kr.ap[:P, :], func=AF.Copy), [kr], [kp2])
        transpose_blocks(kp2, lambda b: kp2.ap[:P, :], 1, P, kpeT, lambda b0, n: kpeT_ap.unsqueeze(1), bw=DR)
        transpose_blocks(cqn, lambda b: cqn.ap[:P, b * 128:(b + 1) * 128], 3, P, cqT,
                         lambda b0, n: cqT.ap[:, b0:b0 + n, 0:P])

        def consUQ(g, c0, w, ps, ps_ap):
            if g < 2:
                evac(qn_b.ap[:P, c0:c0 + w], ps_ap, [ps], [qn_b])
            else:
                evac(qpf.ap[:P, :], ps_ap, [ps], [qpf])
        dense(cqT, lambda c: cqT.ap[:, c, 0:P], 3, Wuq, lambda c, c0, w: Wuq.ap[:, c, c0:c0 + w], P, 1536, consUQ)
        rope(qpf, qp_b, cb, sn, P, MH, 32, mt1, mt2)
        transpose_blocks(qp_b, lambda b: qp_b.ap[:P, b * DR:(b + 1) * DR], MH, P, qpT,
                         lambda b0, n: qpT.ap[:, b0:b0 + n, 0:P], bw=DR)
        transpose_blocks(qn_b, lambda b: qn_b.ap[:P, b * 128:(b + 1) * 128], 8, P, qnT,
                         lambda b0, n: qnT.ap[:, b0:b0 + n, 0:P])
        for q4 in range(4):
            ps = nextS()
            for j in range(4):
                hl = q4 * 4 + j
                hd, lc = hl // 2, hl % 2
                k.ops("pe", [lambda h, hd=hd, lc=lc, ps=ps, j=j: h.matmul(
                    ps.ap[:, j * 128:j * 128 + P], wukT.ap[:, hd, lc * 128:(lc + 1) * 128],
                    qnT.ap[:, hd, 0:P], start=True, stop=True)], [wukT, qnT], [ps])
            src = ps.ap[:, :].rearrange("p (b c) -> p b c", c=128)[:, :, 0:P]
            evac(qlT.ap[:, q4 * 4:q4 * 4 + 4, 0:P], src, [ps], [qlT])

    def mla_tail(t, uTb):
        P = rows(t)
        for half in range(2):
            ps = nextS()
            for j in range(4):
                hd = half * 4 + j
                fns = [lambda h, hd=hd, lc=lc, j=j, ps=ps: h.matmul(
                    ps.ap[:, j * 128:j * 128 + P], Wuv.ap[:, hd * 2 + lc, :], olT.ap[:, hd * 2 + lc, 0:P],
                    start=(lc == 0), stop=(lc == 1)) for lc in range(2)]
                k.ops("pe", fns, [Wuv, olT], [ps])
            src = ps.ap[:, :].rearrange("p (b c) -> p b c", c=128)[:, :, 0:P]
            evac(omT.ap[:, half * 4:half * 4 + 4, 0:P], src, [ps], [omT])
        gated_add(t, P, uTb, omT, lambda c: omT.ap[:, c, 0:P], MH, Wmo, WgB, sgb)

    esMp = ExitStack()
    ckvT = alloc(esMp, "ckvT", [128, 2, TP * 128], BF16)
    kpeT = alloc(esMp, "kpeT", [DR, TP * 128], BF16)
    Vc = alloc(esMp, "Vc", [128, TP, KVL + 16], BF16)
    caus = alloc(esMp, "caus", [128, 128], BF16)
    k.dma("sp", caus.ap[:, :], c_caus[:, :], caus, True)
    k.op("pool", lambda h: h.memset(Vc.ap[:, :, :], 1.0), [], [Vc])
    p_bs = [alloc(esMp, "p_b%d" % i, [128, 2048], BF16) for i in range(2)]
    smS = [alloc(esMp, "smS%d" % i, [128, 8], F32) for i in range(2)]
    pT = alloc(esMp, "pT", [128, 16, 128], BF16)
    def _tileMp(t, um):
        P = 128
        uTb = um.get(t)
        um.load(t + 1)
        mla_front(t, uTb, ckvT.ap[:, :, tok0(t):tok0(t) + P], kpeT.ap[:, tok0(t):tok0(t) + P], Vc.ap[:, t, 0:KVL], Vc,
                  ckvT, kpeT)
        um.prefetch(t + 1)
        nk = (t + 1) * 128
        def scores(hd):
            for g0 in range(0, nk, 512):
                w = min(512, nk - g0)
                ps = S[g0 // 512]
                fns = [
                    lambda h, g0=g0, w=w, ps=ps, hd=hd: h.matmul(ps.ap[:P, 0:w], qlT.ap[:, hd * 2, 0:P],
                                                               ckvT.ap[:, 0, g0:g0 + w], start=True, stop=False),
                    lambda h, g0=g0, w=w, ps=ps, hd=hd: h.matmul(ps.ap[:P, 0:w], qlT.ap[:, hd * 2 + 1, 0:P],
                                                               ckvT.ap[:, 1, g0:g0 + w], start=False, stop=False),
                    lambda h, g0=g0, w=w, ps=ps, hd=hd: h.matmul(
                        ps.ap[:P, 0:w], qpT.ap[:, hd, 0:P], kpeT.ap[:, g0:g0 + w],
                        start=False, stop=True)]
                k.ops("pe", fns, [qlT, ckvT, qpT, kpeT], [ps])

        nsb = (nk + 511) // 512
        sc_ap = S_t[:P, 0:nk]

        def soft(hd):
            p_b, sm = p_bs[hd % 2], smS[hd % 2]
            k.op("dve", lambda h: h.tensor_reduce(out=sm.ap[:P, 2:3], in_=sc_ap, axis=AX.X, op=ALU.max),
                 S[0:nsb], [sm])
            k.op("dve", lambda h: h.tensor_scalar(out=sm.ap[:P, 3:4], in0=sm.ap[:P, 2:3], scalar1=-SCALE,
                                                  scalar2=None, op0=ALU.mult), [sm], [sm])
            k.op("act", lambda h: h.activation(out=p_b.ap[:P, 0:nk], in_=sc_ap, func=AF.Exp,
                                               bias=sm.ap[:P, 3:4], scale=SCALE), S[0:nsb] + [sm], [p_b])
            k.op("pool", lambda h: h.tensor_tensor(out=p_b.ap[:P, nk - 128:nk], in0=p_b.ap[:P, nk - 128:nk],
                                                   in1=caus.ap[:, :], op=ALU.mult), [p_b, caus], [p_b])

        def pv(hd):
            p_b, sm = p_bs[hd % 2], smS[hd % 2]
            transpose_blocks(p_b, lambda b: p_b.ap[:P, b * 128:(b + 1) * 128], t + 1, P, pT,
                             lambda b0, n: pT.ap[:, b0:b0 + n, 0:P])
            fns = [lambda h, kb=kb: h.matmul(O_b.ap[:P, 0:KVL + 1], pT.ap[:, kb, 0:P], Vc.ap[:, kb, 0:KVL + 1],
                                            start=(kb == 0), stop=(kb == t)) for kb in range(t + 1)]
            k.ops("pe", fns, [pT, Vc], [O_b])
            k.op("dve", lambda h: h.reciprocal(out=sm.ap[:P, 5:6], in_=O_b.ap[:P, KVL:KVL + 1]), [O_b], [sm])
            k.op("act", lambda h: h.activation(
                out=ol.ap[:P, hd * 2:hd * 2 + 2, :], in_=O_b.ap[:P, 0:KVL].rearrange("p (lc c) -> p lc c", c=128),
                func=AF.Copy, scale=sm.ap[:P, 5:6]), [O_b, sm], [ol])

        scores(0)
        soft(0)
        for hd in range(MH):
            if hd + 1 < MH:
                scores(hd + 1)
                soft(hd + 1)
            pv(hd)
        transpose_blocks(ol, lambda b: ol.ap[:P, b, :], 16, P, olT, lambda b0, n: olT.ap[:, b0:b0 + n, 0:P])
        mla_tail(t, uTb)
    for t in range(TP):
        _tileMp(t, um)
    k.barrier()
    esMp.close()

    esMs = ExitStack()
    t = TP
    P = PS
    ckvTs = alloc(esMs, "ckvTs", [128, 2, PS], BF16)
    kpeTs = alloc(esMs, "kpeTs", [DR, PS], BF16)
    Vs = alloc(esMs, "Vs", [128, KVL], BF16)
    qlb = alloc(esMs, "qlb", [128, 2, NB, 32], BF16)
    qpb = alloc(esMs, "qpb", [64, NB, 32], BF16)
    big = alloc(esMs, "big", [32, 128], F32)
    pidx = alloc(esMs, "pidx", [128, NB * (NPG // 8)], I32)
    goff = alloc(esMs, "goff", [128, NB * (NPG // 8)], I32)
    gidx = alloc(esMs, "gidx", [128, NB * (NPG // 8)], I32)
    PGC = 8
    assert NPG % 8 == 0
    NCH = NPG // PGC
    assert NCH * PGC == NPG and NCH + 1 <= 9
    QG = (PGC + 3) // 4
    kbc = [alloc(esMs, "kbc%d" % i, [128, PGC, KVL], BF16) for i in range(3)]
    kbp = [alloc(esMs, "kbp%d" % i, [128, PGC, DR], BF16) for i in range(3)]
    KTc = [alloc(esMs, "KTc%d" % i, [128, 3, PGC * 128], BF16) for i in range(2)]
    ps_sb = alloc(esMs, "ps_sb", [32, PGC * 128], BF16)
    pTs = alloc(esMs, "pTs", [128, PGC, 32], BF16)
    oc = alloc(esMs, "oc", [32, 9, KVL], F32)
    mc = alloc(esMs, "mc", [32, 48], F32)
    ob = alloc(esMs, "ob", [32, KVL], F32)
    obb = alloc(esMs, "obb", [32, KVL], BF16)
    snew = alloc(esMs, "snew", [32, PS], F32)
    k.dma("sp", big.ap[:, :], c_big[0:32, :], big, True)
    NBC = NB * (NPG // 8)
    ptv = ptab.rearrange("o (bc j) -> j (o bc)", j=8)
    for j in range(8):
        k.dma("sp", pidx.ap[j * 16:(j + 1) * 16, :], ptv[j].partition_broadcast(16), pidx, True)
    k.dma("sp", goff.ap[:, :], c_goff[:, :], goff, True)
    k.op("dve", lambda h: h.tensor_single_scalar(out=gidx.ap[:, :], in_=pidx.ap[:, :], scalar=4,
                                                 op=ALU.logical_shift_left), [pidx], [gidx])
    k.op("dve", lambda h: h.tensor_tensor(out=gidx.ap[:, :], in0=gidx.ap[:, :], in1=goff.ap[:, :], op=ALU.bitwise_or),
         [gidx, goff], [gidx])
    ckv_rows = c_ckv.rearrange("n (g r) d -> (n g) (r d)", r=8)
    kpe_rows = c_kpe.rearrange("n (g r) d -> (n g) (r d)", r=8)
    uTb = um.get(t)
    mla_front(t, uTb, ckvTs.ap[:, :, :], kpeTs.ap[:, :], Vs.ap[:P, :], Vs, ckvTs, kpeTs)
    for lc in range(2):
        k.op("pool", lambda h, lc=lc: h.tensor_copy(
            out=qlb.ap[:, lc, :, :].rearrange("p b (h i) -> p b h i", i=4),
            in_=qlT.ap[:, :, 0:PS].rearrange("p (h lc) (b i) -> p lc b h i", lc=2, i=4)[:, lc]), [qlT], [qlb])
    k.op("pool", lambda h: h.tensor_copy(
        out=qpb.ap[:, :, :].rearrange("p b (h i) -> p b h i", i=4),
        in_=qpT.ap[:, :, 0:PS].rearrange("p h (b i) -> p b h i", i=4)), [qpT], [qpb])
    seq = [(b, c) for b in range(NB) for c in range(NCH)]

    def phaseA(i):
        b, c = seq[i]
        kc_, kp_ = kbc[i % 3], kbp[i % 3]
        par = i % 2
        KT = KTc[par]
        bc = b * NCH + c
        k.dma("pool", None, None, kc_, True, R=[gidx], fn=lambda h, bc=bc, kc_=kc_: h.indirect_dma_start(
            out=kc_.ap[:, :, :].rearrange("p a d -> p (a d)"), out_offset=None, in_=ckv_rows,
            in_offset=bass.IndirectOffsetOnAxis(ap=gidx.ap[:, bc:bc + 1], axis=0)))
        k.dma("pool", None, None, kp_, True, R=[gidx], fn=lambda h, bc=bc, kp_=kp_: h.indirect_dma_start(
            out=kp_.ap[:, :, :].rearrange("p a d -> p (a d)"), out_offset=None, in_=kpe_rows,
            in_offset=bass.IndirectOffsetOnAxis(ap=gidx.ap[:, bc:bc + 1], axis=0)))
        for q4 in range(QG):
            npg4 = min(4, PGC - q4 * 4)
            for blk in range(3):
                bw = 128 if blk < 2 else 64
                tb = nextT()
                fns = []
                for j in range(npg4):
                    pg = q4 * 4 + j
                    src_ = kc_.ap[:, pg, blk * 128:(blk + 1) * 128] if blk < 2 else kp_.ap[:, pg, :]
                    fns.append(lambda h, j=j, bw=bw, tb=tb, src_=src_: h.transpose(
                        tb.ap[:bw, j * 128:(j + 1) * 128], src_,
                        ident.ap[:, :]))
                k.ops("pe", fns, [kc_, kp_, ident], [tb])
                evac(KT.ap[:bw, blk, q4 * 512:q4 * 512 + npg4 * 128], tb.ap[:bw, 0:npg4 * 128], [tb], [KT])
        nkc = PGC * 128
        for g0 in range(0, nkc, 512):
            w = min(512, nkc - g0)
            ps = S[par * 2 + g0 // 512]
            fns = [
                lambda h, g0=g0, w=w, ps=ps, b=b, KT=KT: h.matmul(ps.ap[:32, 0:w], qlb.ap[:, 0, b, :],
                                                         KT.ap[:, 0, g0:g0 + w], start=True, stop=False),
                lambda h, g0=g0, w=w, ps=ps, b=b, KT=KT: h.matmul(ps.ap[:32, 0:w], qlb.ap[:, 1, b, :],
                                                         KT.ap[:, 1, g0:g0 + w], start=False, stop=False),
                lambda h, g0=g0, w=w, ps=ps, b=b, KT=KT: h.matmul(ps.ap[:32, 0:w], qpb.ap[:, b, :],
                                                         KT.ap[0:64, 2, g0:g0 + w], start=False, stop=True)]
            k.ops("pe", fns, [qlb, qpb, KT], [ps])

    def phaseB(i):
        b, c = seq[i]
        kc_ = kbc[i % 3]
        par = i % 2
        nkc = PGC * 128
        sc_ap = S_t[:32, par * 1024:par * 1024 + nkc]
        scR = [S[par * 2], S[par * 2 + 1]]
        k.op("dve", lambda h, c=c, sc_ap=sc_ap: h.tensor_reduce(out=mc.ap[:, c:c + 1], in_=sc_ap,
                                                               axis=AX.X, op=ALU.max), scR, [mc])
        k.op("dve", lambda h, c=c: h.tensor_scalar(out=mc.ap[:, 20 + c:21 + c], in0=mc.ap[:, c:c + 1],
                                                  scalar1=-SCALE, scalar2=None, op0=ALU.mult), [mc], [mc])
        k.op("act", lambda h, c=c, nkc=nkc, sc_ap=sc_ap: h.activation(
            out=ps_sb.ap[:, 0:nkc], in_=sc_ap, func=AF.Exp, bias=mc.ap[:, 20 + c:21 + c], scale=SCALE,
            accum_out=mc.ap[:, 10 + c:11 + c]), scR + [mc], [ps_sb, mc])

    def pvB(i):
        b, c = seq[i]
        kc_ = kbc[i % 3]
        transpose_blocks(ps_sb, lambda bb: ps_sb.ap[:, bb * 128:(bb + 1) * 128], PGC, 32, pTs,
                         lambda b0, n: pTs.ap[:, b0:b0 + n, :])
        fns = [lambda h, pg=pg, kc_=kc_: h.matmul(O_b.ap[:32, 0:KVL], pTs.ap[:, pg, :], kc_.ap[:, pg, :],
                                                 start=(pg == 0), stop=(pg == PGC - 1)) for pg in range(PGC)]
        k.ops("pe", fns, [pTs, kc_], [O_b])
        evac(oc.ap[:, c, :], O_b.ap[:32, 0:KVL], [O_b], [oc])

    def tailB(b):
        c = NCH
        fns = [
            lambda h, b=b: h.matmul(G_b.ap[:32, 0:PS], qlb.ap[:, 0, b, :], ckvTs.ap[:, 0, :], start=True, stop=False),
            lambda h, b=b: h.matmul(G_b.ap[:32, 0:PS], qlb.ap[:, 1, b, :], ckvTs.ap[:, 1, :], start=False, stop=False),
            lambda h, b=b: h.matmul(G_b.ap[:32, 0:PS], qpb.ap[:, b, :], kpeTs.ap[:, :], start=False, stop=True)]
        k.ops("pe", fns, [qlb, qpb, ckvTs, kpeTs], [G_b])
        k.op("dve", lambda h, b=b: h.tensor_tensor(out=snew.ap[:, :], in0=G_b.ap[:32, 0:PS],
                                                   in1=big.ap[:, 64 - 4 * b:128 - 4 * b], op=ALU.add),
             [G_b, big], [snew])
        k.op("dve", lambda h, c=c: h.tensor_reduce(out=mc.ap[:, c:c + 1], in_=snew.ap[:, :], axis=AX.X, op=ALU.max),
             [snew], [mc])
        k.op("dve", lambda h, c=c: h.tensor_scalar(out=mc.ap[:, 20 + c:21 + c], in0=mc.ap[:, c:c + 1], scalar1=-SCALE,
                                                  scalar2=None, op0=ALU.mult), [mc], [mc])
        k.op("act", lambda h, c=c: h.activation(out=ps_sb.ap[:, 0:PS], in_=snew.ap[:, :], func=AF.Exp,
                                               bias=mc.ap[:, 20 + c:21 + c], scale=SCALE,
                                               accum_out=mc.ap[:, 10 + c:11 + c]), [snew, mc], [ps_sb, mc])
        transpose_blocks(ps_sb, lambda bb: ps_sb.ap[:, 0:PS], 1, 32, pTs, lambda b0, n: pTs.ap[:PS, 0:1, :], bw=PS)
        k.ops("pe", [lambda h: h.matmul(O_b.ap[:32, 0:KVL], pTs.ap[:PS, 0, :], Vs.ap[:PS, :], start=True, stop=True)],
              [pTs, Vs], [O_b])
        evac(oc.ap[:, c, :], O_b.ap[:32, 0:KVL], [O_b], [oc])
        nch = NCH + 1
        k.op("dve", lambda h: h.tensor_reduce(out=mc.ap[:, 30:31], in_=mc.ap[:, 0:nch], axis=AX.X, op=ALU.max),
             [mc], [mc])
        k.op("dve", lambda h: h.tensor_scalar(out=mc.ap[:, 31:32], in0=mc.ap[:, 30:31], scalar1=-SCALE, scalar2=None,
                                              op0=ALU.mult), [mc], [mc])
        k.op("act", lambda h: h.activation(out=mc.ap[:, 20:20 + nch], in_=mc.ap[:, 0:nch], func=AF.Exp,
                                           bias=mc.ap[:, 31:32], scale=SCALE), [mc], [mc])
        k.op("dve", lambda h: h.tensor_tensor(out=mc.ap[:, 34:34 + nch], in0=mc.ap[:, 20:20 + nch],
                                              in1=mc.ap[:, 10:10 + nch], op=ALU.mult), [mc], [mc])
        k.op("dve", lambda h: h.tensor_reduce(out=mc.ap[:, 32:33], in_=mc.ap[:, 34:34 + nch], axis=AX.X, op=ALU.add),
             [mc], [mc])
        k.op("dve", lambda h: h.reciprocal(out=mc.ap[:, 33:34], in_=mc.ap[:, 32:33]), [mc], [mc])
        k.op("dve", lambda h: h.tensor_scalar(out=mc.ap[:, 20:20 + nch], in0=mc.ap[:, 20:20 + nch],
                                              scalar1=mc.ap[:, 33:34], scalar2=None, op0=ALU.mult), [mc], [mc])
        k.op("dve", lambda h: h.tensor_scalar(out=ob.ap[:, :], in0=oc.ap[:, 0, :], scalar1=mc.ap[:, 20:21],
                                              scalar2=None, op0=ALU.mult), [oc, mc], [ob])
        for c2 in range(1, nch):
            k.op("dve", lambda h, c2=c2: h.scalar_tensor_tensor(
                out=ob.ap[:, :], in0=oc.ap[:, c2, :], scalar=mc.ap[:, 20 + c2:21 + c2], in1=ob.ap[:, :],
                op0=ALU.mult, op1=ALU.add), [oc, mc, ob], [ob])
        k.op("act", lambda h: h.activation(out=obb.ap[:, :], in_=ob.ap[:, :], func=AF.Copy), [ob], [obb])
        tb = nextT()
        fns = [lambda h, lc=lc, tb=tb: h.transpose(tb.ap[:, lc * 128:lc * 128 + 32], obb.ap[:, lc * 128:(lc + 1) * 128],
                                                  ident.ap[:32, :32]) for lc in range(2)]
        k.ops("pe", fns, [obb, ident], [tb])
        for lc in range(2):
            dst = olT.ap[:, :, 4 * b:4 * b + 4].rearrange("p (h lc) i -> p h lc i", lc=2)[:, :, lc, :]
            src = tb.ap[:, lc * 128:lc * 128 + 32].rearrange("p (h i) -> p h i", i=4)
            evac(dst, src, [tb], [olT])

    phaseA(0)
    for i, (b, c) in enumerate(seq):
        phaseB(i)
        if i + 1 < len(seq):
            phaseA(i + 1)
        pvB(i)
        if c == NCH - 1:
            tailB(b)
    mla_tail(TP, uTb)
    k.barrier()
    esMs.close()
    esM.close()

    esX = ExitStack()

    def sbX(name, shape, dt):
        return alloc(esX, name, shape, dt)

    Wxq = sbX("Wxq", [128, KC, XH * XD], BF16)
    Wmk = sbX("Wmk", [128, KC, XH * XD], BF16)
    Wmv = sbX("Wmv", [128, KC, XH * XD], BF16)
    Wxo = sbX("Wxo", [128, 2, D], BF16)
    WgC = sbX("WgC", [128, KC, D], BF16)
    um = UMaker(esX, "X")
    um.load(0)
    load_w(Wxq, w_in[:, 3776:4032], D, 256)
    load_w(Wmk, w_mk, D, 256)
    load_w(Wmv, w_mv, D, 256)
    load_w(Wxo, w_xo, XH * XD, D)
    load_w(WgC, w_in[:, 7104 - 1024:7104], D, D)
    mkT = [sbX("mkT%d" % i, [XD, XH, NMEM], BF16) for i in range(NB + 1)]
    mvb = [sbX("mvb%d" % i, [128, 2, XH * XD], BF16) for i in range(NB + 1)]
    mst = [sbX("mst%d" % i, [128, D], F32) for i in range(2)]
    mnb = sbX("mnb", [128, D], BF16)
    mnT = sbX("mnT", [128, KC, NMEM], BF16)
    mkf = [sbX("mkf%d" % i, [128, XH * XD], F32) for i in range(2)]
    mkb = sbX("mkb", [128, 2, XH * XD], BF16)
    smX = sbX("smX", [128, 16], F32)
    gmem_bc = sbX("gmem_bc", [128, D], F32)
    k.dma("sp", gmem_bc.ap[:, :], g_mem[0].partition_broadcast(128), gmem_bc, True)
    bmx = sbX("bmx", [128, NB * PS], BF16)
    k.dma("sp", bmx.ap[:, :], c_bm[:, :], bmx, True)

    def make_mkT(slot):
        tb = nextT()
        fns = []
        for mt in range(2):
            for hh in range(XH):
                j = mt * XH + hh
                fns.append(lambda h, mt=mt, hh=hh, j=j, tb=tb: h.transpose(
                    tb.ap[:XD, j * 128:(j + 1) * 128], mkb.ap[:, mt, hh * XD:(hh + 1) * XD], ident.ap[:, :]))
        k.ops("pe", fns, [mkb, ident], [tb])
        evac(mkT[slot].ap[:, :, :].rearrange("p hh (mt c) -> p mt hh c", c=128),
             tb.ap[:XD, 0:1024].rearrange("p (mt hh c) -> p mt hh c", mt=2, hh=XH), [tb], [mkT[slot]])

    for mt in range(2):
        ms = mst[mt]
        k.dma("sp", ms.ap[:, :], mem_p[mt * 128:(mt + 1) * 128, :], ms, True)
        rstd_of([ms.ap[:, :]], [ms], 128, smX, D)
        k.op("dve", lambda h, ms=ms: h.scalar_tensor_tensor(
            out=mnb.ap[:, :], in0=ms.ap[:, :], scalar=smX.ap[:, 0:1], in1=gmem_bc.ap[:, :],
            op0=ALU.mult, op1=ALU.mult), [ms, smX, gmem_bc], [mnb])
        transpose_blocks(mnb, lambda b: mnb.ap[:, b * 128:(b + 1) * 128], 8, 128, mnT,
                         lambda b0, n, mt=mt: mnT.ap[:, b0:b0 + n, mt * 128:(mt + 1) * 128])
    for mt in range(2):
        for which, (Wm, od) in enumerate(((Wmk, o_mk), (Wmv, o_mv))):
            def consMK(g, c0, w, ps, ps_ap, mt=mt, which=which, od=od):
                f = mkf[which]
                evac(f.ap[:, :], ps_ap, [ps], [f])
                k.dma("pool", od[mt * 128:(mt + 1) * 128, :], f.ap[:, :], f, False)
                if which == 0:
                    evac(mkb.ap[:, mt, :], f.ap[:, :], [f], [mkb])
                else:
                    evac(mvb[NB].ap[:, mt, :], f.ap[:, :], [f], [mvb[NB]])
            dense(mnT, lambda c, mt=mt: mnT.ap[:, c, mt * 128:(mt + 1) * 128], KC, Wm,
                  lambda c, c0, w, Wm=Wm: Wm.ap[:, c, c0:c0 + w], 128, 256, consMK)
    make_mkT(NB)
    for b in range(NB):
        ms = mst[b % 2]
        k.dma("sp", ms.ap[:, 0:512].rearrange("p (mt c) -> p mt c", c=256),
              cmk[b * NMEM:(b + 1) * NMEM, :].rearrange("(mt p) c -> p mt c", p=128), ms, True)
        k.dma("sp", ms.ap[:, 512:1024].rearrange("p (mt c) -> p mt c", c=256),
              cmv[b * NMEM:(b + 1) * NMEM, :].rearrange("(mt p) c -> p mt c", p=128), ms, True)
        evac(mkb.ap[:, :, :], ms.ap[:, 0:512].rearrange("p (mt c) -> p mt c", c=256), [ms], [mkb])
        evac(mvb[b].ap[:, :, :], ms.ap[:, 512:1024].rearrange("p (mt c) -> p mt c", c=256), [ms], [mvb[b]])
        make_mkT(b)
    xq_b = sbX("xq_b", [128, XH * XD], BF16)
    xqT = sbX("xqT", [XD, XH, 128], BF16)
    xqTm = sbX("xqTm", [XD, XH, NB, PS], BF16)
    p_x = sbX("p_x", [128, XH * NMEM], BF16)
    pTx = sbX("pTx", [128, 8, 128], BF16)
    pTxm = sbX("pTxm", [128, 8, NB, PS], BF16)
    ox = sbX("ox", [128, XH * XD], BF16)
    oxT = sbX("oxT", [128, 2, 128], BF16)
    sgx = sbX("sgx", [128, D], F32)
    def _tileX(t, um):
        P = rows(t)
        smp = (t == TP)
        uTb = um.get(t)
        um.load(t + 1)

        def consXQ(g, c0, w, ps, ps_ap, P=P):
            evac(xq_b.ap[:P, :], ps_ap, [ps], [xq_b])
        dense(uTb, lambda c, uTb=uTb, P=P: uTb.ap[:, c, 0:P], KC, Wxq, lambda c, c0, w: Wxq.ap[:, c, c0:c0 + w],
              P, 256, consXQ)
        transpose_blocks(xq_b, lambda b, P=P: xq_b.ap[:P, b * XD:(b + 1) * XD], XH, P, xqT,
                         lambda b0, n, P=P: xqT.ap[:, b0:b0 + n, 0:P], bw=XD)
        um.prefetch(t + 1)
        if smp:
            for hh in range(XH):
                k.op("pool", lambda h, hh=hh: h.tensor_tensor(
                    out=xqTm.ap[:, hh, :, :], in0=bmx.ap[:XD, :].rearrange("p (b c) -> p b c", c=PS),
                    in1=xqT.ap[:, hh:hh + 1, 0:PS].to_broadcast([XD, NB, PS]), op=ALU.mult), [bmx, xqT], [xqTm])
        for hd in range(XH):
            ps = S[hd // 2]
            o_ap = ps.ap[:P, (hd % 2) * NMEM:(hd % 2 + 1) * NMEM]
            if not smp:
                k.ops("pe", [lambda h, o_ap=o_ap, hd=hd, P=P: h.matmul(
                    o_ap, xqT.ap[:, hd, 0:P], mkT[NB].ap[:, hd, :],
                    start=True, stop=True)], [xqT, mkT[NB]], [ps])
            else:
                fns = [lambda h, o_ap=o_ap, hd=hd, b=b: h.matmul(
                    o_ap, xqTm.ap[:, hd, b, :], mkT[b].ap[:, hd, :],
                    start=(b == 0), stop=(b == NB - 1)) for b in range(NB)]
                k.ops("pe", fns, [xqTm] + mkT[0:NB], [ps])
        sall = S_t[:P, 0:XH * NMEM].rearrange("p (h m) -> p h m", m=NMEM)
        k.op("dve", lambda h, sall=sall, P=P: h.tensor_reduce(out=smX.ap[:P, 0:4], in_=sall, axis=AX.X, op=ALU.max),
             [S[0], S[1]], [smX])
        k.op("dve", lambda h, P=P: h.tensor_scalar(out=smX.ap[:P, 4:8], in0=smX.ap[:P, 0:4], scalar1=-XSCALE,
                                                   scalar2=None, op0=ALU.mult), [smX], [smX])
        for hd in range(XH):
            ps = S[hd // 2]
            k.op("act", lambda h, hd=hd, ps=ps, P=P: h.activation(
                out=p_x.ap[:P, hd * NMEM:(hd + 1) * NMEM], in_=ps.ap[:P, (hd % 2) * NMEM:(hd % 2 + 1) * NMEM],
                func=AF.Exp, bias=smX.ap[:P, 4 + hd:5 + hd], scale=XSCALE, accum_out=smX.ap[:P, 10 + hd:11 + hd]),
                [ps, smX], [p_x, smX])
        k.op("dve", lambda h, P=P: h.reciprocal(out=smX.ap[:P, 0:4], in_=smX.ap[:P, 10:14]), [smX], [smX])
        transpose_blocks(p_x, lambda b, P=P: p_x.ap[:P, b * 128:(b + 1) * 128], 8, P, pTx,
                         lambda b0, n, P=P: pTx.ap[:, b0:b0 + n, 0:P])
        if smp:
            for j in range(8):
                en = ("pool", "dve")[j % 2]
                k.op(en, lambda h, j=j: h.tensor_tensor(
                    out=pTxm.ap[:, j, :, :], in0=bmx.ap[:, :].rearrange("p (b c) -> p b c", c=PS),
                    in1=pTx.ap[:, j:j + 1, 0:PS].to_broadcast([128, NB, PS]), op=ALU.mult), [bmx, pTx], [pTxm])
        for hd in range(XH):
            o_ap = O_b.ap[:P, hd * XD:(hd + 1) * XD]
            if not smp:
                fns = [lambda h, o_ap=o_ap, hd=hd, mc_=mc_, P=P: h.matmul(
                    o_ap, pTx.ap[:, hd * 2 + mc_, 0:P], mvb[NB].ap[:, mc_, hd * XD:(hd + 1) * XD],
                    start=(mc_ == 0), stop=(mc_ == 1)) for mc_ in range(2)]
                k.ops("pe", fns, [pTx, mvb[NB]], [O_b])
            else:
                fns = [lambda h, o_ap=o_ap, hd=hd, mc_=mc_, b=b: h.matmul(
                    o_ap, pTxm.ap[:, hd * 2 + mc_, b, :], mvb[b].ap[:, mc_, hd * XD:(hd + 1) * XD],
                    start=(b == 0 and mc_ == 0), stop=(b == NB - 1 and mc_ == 1))
                    for b in range(NB) for mc_ in range(2)]
                k.ops("pe", fns, [pTxm] + mvb[0:NB], [O_b])
        for hd in range(XH):
            k.op("act", lambda h, hd=hd, P=P: h.activation(out=ox.ap[:P, hd * XD:(hd + 1) * XD],
                                                          in_=O_b.ap[:P, hd * XD:(hd + 1) * XD], func=AF.Copy,
                                                          scale=smX.ap[:P, hd:hd + 1]), [O_b, smX], [ox])
        transpose_blocks(ox, lambda b, P=P: ox.ap[:P, b * 128:(b + 1) * 128], 2, P, oxT,
                         lambda b0, n, P=P: oxT.ap[:, b0:b0 + n, 0:P])
        gated_add(t, P, uTb, oxT, lambda c, P=P: oxT.ap[:, c, 0:P], 2, Wxo, WgC, sgx)
    for t in range(NT):
        _tileX(t, um)
    k.barrier()
    esX.close()

    esF = ExitStack()

    def sbF(name, shape, dt):
        return alloc(esF, name, shape, dt)

    Wout = sbF("Wout", [128, KC, D], BF16)
    Wfg = sbF("Wfg", [128, KC, DFF], BF16)
    Wfu = sbF("Wfu", [128, KC, DFF], BF16)
    xtF = [sbF("xtF%d" % i, [128, D], F32) for i in range(2)]
    f_b = sbF("f_b", [128, D], BF16)
    fT = sbF("fT", [128, KC, 128], BF16)
    act_b = sbF("act_b", [128, DFF], BF16)
    actT = sbF("actT", [128, FC, 128], BF16)
    sgt = sbF("sgt", [128, 512], F32)
    gpo = sbF("gpo", [128, D], F32)
    gfo = sbF("gfo", [128, D], F32)
    gfi = sbF("gfi", [128, D], F32)
    smF = sbF("smF", [128, 16], F32)
    wdr = [sbF("wdr%d" % i, [128, 2, D], BF16) for i in range(3)]
    scr = Buf(None, "wd_scr")
    k.dma("sp", xtF[0].ap[:, :], x_all[0:128, :], xtF[0], True)
    load_w(Wout, w_o, D, D)
    k.dma("sp", gpo.ap[:, :], g_post[0].partition_broadcast(128), gpo, True)
    k.dma("sp", gfo.ap[:, :], g_fpost[0].partition_broadcast(128), gfo, True)
    k.dma("sp", gfi.ap[:, :], g_fpre[0].partition_broadcast(128), gfi, True)
    for c in range(FC):
        k.dma("pool", wd_scr[c * 128:(c + 1) * 128, :], w_fd[c * 128:(c + 1) * 128, :], scr, True)
    load_w(Wfg, w_fg, D, DFF)
    load_w(Wfu, w_fu, D, DFF)
    wd_i = [0]

    def _tileF(t):
        P = rows(t)
        if t + 1 < NT:
            P1 = rows(t + 1)
            k.dma("sp", xtF[(t + 1) % 2].ap[:P1, :], x_all[tok0(t + 1):tok0(t + 1) + P1, :], xtF[(t + 1) % 2], True)
        xb = xtF[t % 2]
        transpose_blocks(mix[t], lambda b, t=t, P=P: mix[t].ap[:P, b * 128:(b + 1) * 128], 8, P, fT,
                         lambda b0, n, P=P: fT.ap[:, b0:b0 + n, 0:P])
        res = dense(fT, lambda c, P=P: fT.ap[:, c, 0:P], KC, Wout, lambda c, c0, w: Wout.ap[:, c, c0:c0 + w], P, D,
                    None, hold=True)
        rstd_of([r[4] for r in res], [r[3] for r in res], P, smF, D, col=0)
        for (g, c0, w, ps, ps_ap) in res:
            k.op("dve", lambda h, c0=c0, w=w, ps_ap=ps_ap, P=P: h.scalar_tensor_tensor(
                out=sgt.ap[:P, 0:w], in0=ps_ap, scalar=smF.ap[:P, 0:1], in1=gpo.ap[:P, c0:c0 + w],
                op0=ALU.mult, op1=ALU.mult), [ps, smF, gpo], [sgt])
            k.op("pool", lambda h, c0=c0, w=w, xb=xb, P=P: h.tensor_tensor(
                out=xb.ap[:P, c0:c0 + w], in0=xb.ap[:P, c0:c0 + w], in1=sgt.ap[:P, 0:w], op=ALU.add),
                [sgt, xb], [xb])
        rstd_of([xb.ap[:P, :]], [xb], P, smF, D, col=1)
        k.op("dve", lambda h, xb=xb, P=P: h.scalar_tensor_tensor(
            out=f_b.ap[:P, :], in0=xb.ap[:P, :], scalar=smF.ap[:P, 1:2], in1=gfi.ap[:P, :],
            op0=ALU.mult, op1=ALU.mult), [xb, smF, gfi], [f_b])
        transpose_blocks(f_b, lambda b, P=P: f_b.ap[:P, b * 128:(b + 1) * 128], 8, P, fT,
                         lambda b0, n, P=P: fT.ap[:, b0:b0 + n, 0:P])
        for c0 in range(0, DFF, 512):
            w = min(512, DFF - c0)
            pg_, pu_ = nextS(), nextS()
            for (ps, Wm) in ((pg_, Wfg), (pu_, Wfu)):
                fns = [lambda h, c=c, c0=c0, w=w, ps=ps, Wm=Wm, P=P: h.matmul(
                    ps.ap[:P, 0:w], fT.ap[:, c, 0:P], Wm.ap[:, c, c0:c0 + w], start=(c == 0), stop=(c == KC - 1))
                    for c in range(KC)]
                k.ops("pe", fns, [fT, Wm], [ps])
            k.op("act", lambda h, w=w, pg_=pg_, P=P: h.activation(out=sgt.ap[:P, 0:w], in_=pg_.ap[:P, 0:w],
                                                                 func=AF.Silu), [pg_], [sgt])
            k.op("dve", lambda h, c0=c0, w=w, pu_=pu_, P=P: h.tensor_tensor(
                out=act_b.ap[:P, c0:c0 + w], in0=pu_.ap[:P, 0:w], in1=sgt.ap[:P, 0:w], op=ALU.mult),
                [pu_, sgt], [act_b])
        transpose_blocks(act_b, lambda b, P=P: act_b.ap[:P, b * 128:(b + 1) * 128], FC, P, actT,
                         lambda b0, n, P=P: actT.ap[:, b0:b0 + n, 0:P])
        pa, pb_ = nextS(), nextS()
        for cc in range(FC // 2):
            wb = wdr[wd_i[0] % 3]
            wd_i[0] += 1
            k.dma("sp", wb.ap[:, :, :], wd_scr[cc * 256:(cc + 1) * 256, :].rearrange("(j p) n -> p j n", p=128), wb,
                  True, R=[scr])
            for (ps, n0) in ((pa, 0), (pb_, 512)):
                fns = [lambda h, j=j, cc=cc, ps=ps, n0=n0, wb=wb, P=P: h.matmul(
                    ps.ap[:P, 0:512], actT.ap[:, cc * 2 + j, 0:P], wb.ap[:, j, n0:n0 + 512],
                    start=(cc == 0 and j == 0), stop=(cc == FC // 2 - 1 and j == 1)) for j in range(2)]
                k.ops("pe", fns, [actT, wb], [ps])
        rstd_of([pa.ap[:P, 0:512], pb_.ap[:P, 0:512]], [pa, pb_], P, smF, D, col=2)
        for (ps, n0) in ((pa, 0), (pb_, 512)):
            k.op("dve", lambda h, ps=ps, n0=n0, P=P: h.scalar_tensor_tensor(
                out=sgt.ap[:P, 0:512], in0=ps.ap[:P, 0:512], scalar=smF.ap[:P, 2:3], in1=gfo.ap[:P, n0:n0 + 512],
                op0=ALU.mult, op1=ALU.mult), [ps, smF, gfo], [sgt])
            k.op("pool", lambda h, n0=n0, xb=xb, P=P: h.tensor_tensor(
                out=xb.ap[:P, n0:n0 + 512], in0=xb.ap[:P, n0:n0 + 512], in1=sgt.ap[:P, 0:512], op=ALU.add),
                [sgt, xb], [xb])
        k.dma("pool", y_all[tok0(t):tok0(t) + P, :], xb.ap[:P, :], xb, False)
    for t in range(NT):
        _tileF(t)
    k.finish()
    k.emit()
    esF.close()
    es.close()
    return nc


_CACHE = {}


def _consts(TP, NPG, NB):
    PS = NB * DEC_SEQ
    SEQ = TP * 128
    past = NPG * 128
    pos = np.concatenate([np.arange(SEQ, dtype=np.float32),
                          np.tile(past + np.arange(DEC_SEQ, dtype=np.float32), NB)]).astype(np.float32)

    def tables(half, nh):
        inv = (ROPE_BASE ** (-np.arange(half, dtype=np.float32) / half)).astype(np.float32)
        ang = (pos[:, None] * inv[None, :]).astype(np.float32)
        cos, sin = np.cos(ang).astype(np.float32), np.sin(ang).astype(np.float32)
        c2 = np.concatenate([cos, cos], axis=1)
        s2 = np.concatenate([-sin, sin], axis=1)
        return np.tile(c2, (1, nh)).astype(np.float32), np.tile(s2, (1, nh)).astype(np.float32)

    cosR, sinR = tables(64, RH)
    cosM, sinM = tables(32, MH)
    gam = np.array([1.0 - 2.0 ** (-5.0 - h) for h in range(RH)], dtype=np.float64)
    sc = DK ** -0.5
    i = np.arange(128)
    dtp = np.zeros((128, RH, 128), np.float64)
    qdp = np.zeros((128, RH, 128), np.float64)
    kdp = np.zeros((128, RH, DK), np.float64)
    dts = np.zeros((128, RH, 128), np.float64)
    qds = np.zeros((128, RH, 128), np.float64)
    kds = np.zeros((128, RH, DK), np.float64)
    for h in range(RH):
        diff = i[None, :] - i[:, None]
        dtp[:, h, :] = np.where(diff >= 0, gam[h] ** np.maximum(diff, 0), 0.0) * sc
        qdp[:, h, :] = (gam[h] ** (i + 1.0))[:, None]
        kdp[:, h, :] = (gam[h] ** (127.0 - i))[:, None] * sc
        ts = np.arange(PS)
        bj, jj = ts // DEC_SEQ, ts % DEC_SEQ
        same = bj[:, None] == bj[None, :]
        dl = jj[None, :] - jj[:, None]
        dts[:PS, h, :PS] = np.where(same & (dl >= 0), gam[h] ** np.maximum(dl, 0), 0.0) * sc
        qds[:PS, h, :] = (gam[h] ** (jj + 1.0))[:, None]
        kds[:PS, h, :] = (gam[h] ** (DEC_SEQ - 1.0 - jj))[:, None] * sc
    bm = np.zeros((128, NB, PS), np.float32)
    for b in range(NB):
        bm[:, b, b * DEC_SEQ:(b + 1) * DEC_SEQ] = 1.0
    bmr = np.zeros((128, NB), np.float32)
    for tkn in range(PS):
        bmr[tkn, tkn // DEC_SEQ] = 1.0
    caus = np.where(i[None, :] <= i[:, None], 1.0, 0.0).astype(np.float32).astype(ml_dtypes.bfloat16)
    big = np.full((128, 128), NEG, np.float32)
    for h in range(MH):
        for ii in range(DEC_SEQ):
            big[h * DEC_SEQ + ii, 64:64 + ii + 1] = 0.0
    f = lambda a: np.ascontiguousarray(a.reshape(a.shape[0], -1).astype(np.float32))
    goff = np.tile((np.arange(128) % 16).astype(np.int32)[:, None], (1, NB * (NPG // 8)))
    return dict(c_goff=goff, c_ident=np.eye(128, dtype=np.float32).astype(ml_dtypes.bfloat16),
                c_cosR=cosR, c_sinR=sinR, c_cosM=cosM, c_sinM=sinM,
                c_dtp=f(dtp), c_dts=f(dts), c_qdp=f(qdp), c_qds=f(qds), c_kdp=f(kdp), c_kds=f(kds),
                c_bm=f(bm).astype(ml_dtypes.bfloat16), c_bmr=bmr.astype(ml_dtypes.bfloat16), c_caus=caus, c_big=big)


def kernel(x_prompt, x_sample, mem_prompt, cache_ckv, cache_kpe, page_table, state_ret,
           cache_mem_k, cache_mem_v, norm_mix_pre, norm_mix_post, norm_ffn_pre, norm_ffn_post,
           norm_mem, norm_q_lat, norm_kv_lat, w_in, w_uq, w_uk, w_uv, w_mem_k, w_mem_v,
           w_ret_o, w_mla_o, w_x_o, w_out, w_ffn_gate, w_ffn_up, w_ffn_down, _prepare_only=False):
    A = lambda a: np.ascontiguousarray(np.asarray(a))
    x_prompt, x_sample, mem_prompt = A(x_prompt), A(x_sample), A(mem_prompt)
    B, SEQ, _ = x_prompt.shape
    DB = x_sample.shape[0]
    NC = B
    NB = DB // NC
    TP = SEQ // 128
    NPOOL = cache_ckv.shape[1]
    NPG = page_table.shape[1]
    PS = NB * DEC_SEQ
    key = (TP, NPG, NPOOL, NB)
    if key not in _CACHE:
        _CACHE[key] = build(*key)
    nc = _CACHE[key]
    consts = _consts(TP, NPG, NB)
    ckv = A(cache_ckv)[0]
    kpe = A(cache_kpe)[0]
    col = lambda a: A(a)[0].reshape(-1, 1)
    row = lambda a: A(a)[0].reshape(1, -1)
    shared = dict(
        c_ckv=ckv, c_kpe=kpe,
        g_pre=row(norm_mix_pre), g_post=row(norm_mix_post), g_fpre=row(norm_ffn_pre), g_fpost=row(norm_ffn_post),
        g_mem=row(norm_mem), g_ql=row(norm_q_lat), g_kv=row(norm_kv_lat),
        w_in=A(w_in)[0], w_uq=A(w_uq)[0], w_uk=A(w_uk)[0].reshape(MH * KVL, DN),
        w_uv=A(w_uv)[0].reshape(MH * KVL, DVH), w_mk=A(w_mem_k)[0], w_mv=A(w_mem_v)[0], w_ro=A(w_ret_o)[0],
        w_mo=A(w_mla_o)[0], w_xo=A(w_x_o)[0], w_o=A(w_out)[0], w_fg=A(w_ffn_gate)[0], w_fu=A(w_ffn_up)[0],
        w_fd=A(w_ffn_down)[0], **consts)
    pt = A(page_table).astype(np.int32)
    st = A(state_ret)[0]
    mk = A(cache_mem_k)[0]
    mv = A(cache_mem_v)[0]
    in_maps = []
    for c in range(NC):
        sl = slice(c * NB, (c + 1) * NB)
        m = dict(shared)
        m["x_all"] = np.concatenate([x_prompt[c], x_sample[sl].reshape(PS, D)], axis=0)
        m["mem_p"] = mem_prompt[c]
        m["ptab"] = pt[sl].reshape(1, NB * NPG)
        m["st_in"] = st[sl].reshape(NB * RH * DK, DV)
        m["cmk"] = mk[sl].reshape(NB * NMEM, XH * XD)
        m["cmv"] = mv[sl].reshape(NB * NMEM, XH * XD)
        in_maps.append(m)
    if _prepare_only:
        return nc, in_maps, (NC, NB, SEQ)
    res = run_bass_kernel_spmd(nc, in_maps, core_ids=list(range(NC))).results
    return _assemble(res, NC, NB, SEQ)


def _assemble(res, NC, NB, SEQ):
    f32 = np.float32
    y_p = np.stack([res[c]["y_all"][:SEQ] for c in range(NC)]).astype(f32)
    y_s = np.concatenate([res[c]["y_all"][SEQ:].reshape(NB, DEC_SEQ, D) for c in range(NC)]).astype(f32)
    ckv_p = np.stack([res[c]["o_ckv"][:SEQ] for c in range(NC)])[None].astype(f32)
    kpe_p = np.stack([res[c]["o_kpe"][:SEQ] for c in range(NC)])[None].astype(f32)
    ckv_s = np.concatenate([res[c]["o_ckv"][SEQ:].reshape(NB, DEC_SEQ, KVL) for c in range(NC)])[None].astype(f32)
    kpe_s = np.concatenate([res[c]["o_kpe"][SEQ:].reshape(NB, DEC_SEQ, DR) for c in range(NC)])[None].astype(f32)
    ret_p = np.stack([res[c]["o_retp"].reshape(RH, DK, DV) for c in range(NC)])[None].astype(f32)
    ret_s = np.concatenate([res[c]["o_rets"].reshape(NB, RH, DK, DV) for c in range(NC)])[None].astype(f32)
    mk_p = np.stack([res[c]["o_mk"].reshape(NMEM, XH, XD) for c in range(NC)])[None].astype(f32)
    mv_p = np.stack([res[c]["o_mv"].reshape(NMEM, XH, XD) for c in range(NC)])[None].astype(f32)
    return (y_p, y_s, ckv_p, kpe_p, ckv_s, kpe_s, ret_p, ret_s, mk_p, mv_p)
```

```python
from contextlib import ExitStack
import numpy as np
import ml_dtypes
import concourse.bass as bass
import concourse.mybir as mybir
from concourse.bass_utils import run_bass_kernel_spmd

F32 = mybir.dt.float32
BF16 = mybir.dt.bfloat16
I32 = mybir.dt.int32
AF = mybir.ActivationFunctionType
ALU = mybir.AluOpType
AX = mybir.AxisListType

D = 1024
KC = 8
RH, DK, DV = 4, 128, 256
MH, QL, KVL, DN, DR, DVH = 8, 384, 256, 128, 64, 128
NMEM, XH, XD = 256, 4, 64
DFF = 2816
FC = DFF // 128
EPS = 1e-6
DEC_SEQ = 4
NEG = -1.0e30
SCALE = float((DN + DR) ** -0.5)
XSCALE = float(XD ** -0.5)
ROPE_BASE = 10000.0
SAME_ENGINE_SYNC = True


class Buf:
    def __init__(self, ap, name):
        self.ap = ap
        self.name = name
        self.w = None
        self.r = {}
        self.dsem = {}
        self.dcnt = {}


class Eng:
    def __init__(self, name, sem, is_pe=False):
        self.name = name
        self.sem = sem
        self.cnt = 0
        self.ops = []
        self.waited = {}
        self.is_pe = is_pe


class K:
    def __init__(self, nc, es):
        self.nc = nc
        self.es = es
        self.E = {}
        for n in ("pe", "act", "dve", "pool", "sp"):
            sem = es.enter_context(nc.semaphore("s_" + n))
            self.E[n] = Eng(n, sem, is_pe=(n == "pe"))
        self.stores = []
        self.nrec = 0
        self.cut = 0

    def _skip(self):
        self.nrec += 1
        return self.cut and self.nrec > self.cut

    def _wait(self, e, s, v):
        if e.waited.get(id(s), 0) >= v:
            return
        e.waited[id(s)] = v
        e.ops.append((lambda h, s=s, v=v: h.wait_ge(s, v), None))

    def _deps(self, e, R, W):
        deps = {}

        def add(sp):
            s, v = sp
            if id(s) not in deps or deps[id(s)][1] < v:
                deps[id(s)] = (s, v)
        for b in R:
            if b.w is not None:
                add(b.w)
        for b in W:
            if b.w is not None:
                add(b.w)
            for sp in b.r.values():
                add(sp)
        for (s, v) in deps.values():
            if s is e.sem and (e.is_pe or not SAME_ENGINE_SYNC):
                continue
            self._wait(e, s, v)

    def _done(self, sp, R, W):
        s, v = sp
        for b in R:
            if id(s) not in b.r or b.r[id(s)][1] < v:
                b.r[id(s)] = (s, v)
        for b in W:
            b.w = sp
            b.r = {}

    def op(self, en, fn, R=(), W=()):
        self.ops(en, [fn], R, W)

    def ops(self, en, fns, R=(), W=()):
        if self._skip():
            return
        e = self.E[en]
        self._deps(e, R, W)
        for f in fns[:-1]:
            e.ops.append((f, None))
        e.cnt += 1
        e.ops.append((fns[-1], (e.sem, 1)))
        self._done((e.sem, e.cnt), R, W)

    def dma(self, q, out_ap, in_ap, sbuf, load, R=(), W=(), pre=None, fn=None):
        if self._skip():
            return
        e = self.E[q]
        if q not in sbuf.dsem:
            sbuf.dsem[q] = self.es.enter_context(self.nc.semaphore("d%s_%s" % (q, sbuf.name)))
            sbuf.dcnt[q] = 0
        Rl = list(R) + ([] if load else [sbuf])
        Wl = list(W) + ([sbuf] if load else [])
        self._deps(e, Rl, Wl)
        sbuf.dcnt[q] += 16
        if pre is not None:
            e.ops.append((pre, None))
        if fn is None:
            fn = (lambda h: h.dma_start(out=out_ap, in_=in_ap))
        e.ops.append((fn, (sbuf.dsem[q], 16)))
        sp = (sbuf.dsem[q], sbuf.dcnt[q])
        self._done(sp, Rl, Wl)
        if not load:
            self.stores.append(sp)

    def _store_points(self):
        last = {}
        for (s, v) in self.stores:
            if id(s) not in last or last[id(s)][1] < v:
                last[id(s)] = (s, v)
        return list(last.values())

    def barrier(self):
        pts = [(eng.sem, eng.cnt) for eng in self.E.values() if eng.cnt] + self._store_points()
        for eng in self.E.values():
            for (s, v) in pts:
                if s is eng.sem:
                    continue
                self._wait(eng, s, v)

    def finish(self):
        e = self.E["sp"]
        for (s, v) in self._store_points():
            self._wait(e, s, v)
        for n, eng in self.E.items():
            if n != "sp" and eng.cnt:
                self._wait(e, eng.sem, eng.cnt)

    def emit(self):
        nc = self.nc
        with nc.Block() as block:
            def run(eng):
                def f(h):
                    for fn, inc in eng.ops:
                        ins = fn(h)
                        if inc is not None:
                            ins.then_inc(inc[0], inc[1])
                return f
            block.tensor(run(self.E["pe"]))
            block.scalar(run(self.E["act"]))
            block.vector(run(self.E["dve"]))
            block.gpsimd(run(self.E["pool"]))
            block.sync(run(self.E["sp"]))


def build(TP, NPG, NPOOL, NB):
    PS = NB * DEC_SEQ
    assert PS == 64
    NT = TP + 1
    TALL = TP * 128 + PS
    nc = bass.Bass("TRN2", target_bir_lowering=False)
    es = ExitStack()

    def din(name, shape, dt=F32):
        return nc.dram_tensor(name, list(shape), dt, kind="ExternalInput").ap()

    def dout(name, shape, dt=F32):
        return nc.dram_tensor(name, list(shape), dt, kind="ExternalOutput").ap()

    x_all = din("x_all", [TALL, D])
    mem_p = din("mem_p", [NMEM, D])
    c_ckv = din("c_ckv", [NPOOL, 128, KVL])
    c_kpe = din("c_kpe", [NPOOL, 128, DR])
    ptab = din("ptab", [1, NB * NPG], I32)
    st_in = din("st_in", [NB * RH * DK, DV])
    cmk = din("cmk", [NB * NMEM, XH * XD])
    cmv = din("cmv", [NB * NMEM, XH * XD])
    g_pre = din("g_pre", [1, D])
    g_post = din("g_post", [1, D])
    g_fpre = din("g_fpre", [1, D])
    g_fpost = din("g_fpost", [1, D])
    g_mem = din("g_mem", [1, D])
    g_ql = din("g_ql", [1, QL])
    g_kv = din("g_kv", [1, KVL])
    w_in = din("w_in", [D, 7104])
    w_uq = din("w_uq", [QL, MH * (DN + DR)])
    w_uk = din("w_uk", [MH * KVL, DN])
    w_uv = din("w_uv", [MH * KVL, DVH])
    w_mk = din("w_mk", [D, XH * XD])
    w_mv = din("w_mv", [D, XH * XD])
    w_ro = din("w_ro", [RH * DV, D])
    w_mo = din("w_mo", [MH * DVH, D])
    w_xo = din("w_xo", [XH * XD, D])
    w_o = din("w_o", [D, D])
    w_fg = din("w_fg", [D, DFF])
    w_fu = din("w_fu", [D, DFF])
    w_fd = din("w_fd", [DFF, D])
    c_ident = din("c_ident", [128, 128], BF16)
    c_cosR = din("c_cosR", [TALL, 512])
    c_sinR = din("c_sinR", [TALL, 512])
    c_cosM = din("c_cosM", [TALL, 512])
    c_sinM = din("c_sinM", [TALL, 512])
    c_dtp = din("c_dtp", [128, RH * 128])
    c_dts = din("c_dts", [128, RH * 128])
    c_qdp = din("c_qdp", [128, RH * 128])
    c_qds = din("c_qds", [128, RH * 128])
    c_kdp = din("c_kdp", [128, RH * DK])
    c_kds = din("c_kds", [128, RH * DK])
    c_bm = din("c_bm", [128, NB * PS], BF16)
    c_bmr = din("c_bmr", [128, NB], BF16)
    c_caus = din("c_caus", [128, 128], BF16)
    c_goff = din("c_goff", [128, NB * (NPG // 8)], I32)
    c_big = din("c_big", [128, 128])

    y_all = dout("y_all", [TALL, D])
    o_ckv = dout("o_ckv", [TALL, KVL])
    o_kpe = dout("o_kpe", [TALL, DR])
    o_retp = dout("o_retp", [RH * DK, DV])
    o_rets = dout("o_rets", [NB * RH * DK, DV])
    o_mk = dout("o_mk", [NMEM, XH * XD])
    o_mv = dout("o_mv", [NMEM, XH * XD])
    wd_scr = nc.dram_tensor("wd_scr", [DFF, D], BF16, kind="Internal").ap()

    k = K(nc, es)
    es.enter_context(nc.allow_low_precision("bf16 matmul operands, fp32 accumulation"))
    es.enter_context(nc.allow_non_contiguous_dma("tiny per-partition gain column loads"))

    def alloc(scope, name, shape, dt):
        return Buf(scope.enter_context(nc.sbuf_tensor(name, list(shape), dt)), name)

    ident = alloc(es, "ident", [128, 128], BF16)
    mix_t = es.enter_context(nc.sbuf_tensor("mix", [128, NT, D], BF16))
    mix = [Buf(mix_t[:, t, :], "mix%d" % t) for t in range(NT)]
    junk = alloc(es, "junk", [128, 1024], F32)
    gpre_bc = alloc(es, "gpre_bc", [128, D], F32)
    epsb = alloc(es, "epsb", [128, 1], F32)
    k.op("dve", lambda h: h.memset(epsb.ap[:, :], EPS), [], [epsb])

    S_t = es.enter_context(nc.psum_tensor("ps_s", [128, 2048], F32))
    S = [Buf(S_t[:, i * 512:(i + 1) * 512], "S%d" % i) for i in range(4)]
    T_t = es.enter_context(nc.psum_tensor("ps_t", [128, 2048], BF16))
    Tt = [Buf(T_t[:, i * 1024:(i + 1) * 1024], "T%d" % i) for i in range(2)]
    O_b = Buf(es.enter_context(nc.psum_tensor("ps_o", [128, 512], F32)), "O")
    G_b = Buf(es.enter_context(nc.psum_tensor("ps_g", [128, 512], F32)), "G")
    s_i = [0]
    t_i = [0]

    def nextS():
        s_i[0] = (s_i[0] + 1) % 4
        return S[s_i[0]]

    def nextT():
        t_i[0] = (t_i[0] + 1) % 2
        return Tt[t_i[0]]

    k.dma("sp", ident.ap[:, :], c_ident[:, :], ident, True)
    k.dma("sp", gpre_bc.ap[:, :], g_pre[0].partition_broadcast(128), gpre_bc, True)

    def rows(t):
        return 128 if t < TP else PS

    def tok0(t):
        return t * 128

    ev_i = [0]

    def evac(out_ap, in_ap, R, W, scale=None, eng=None):
        ev_i[0] += 1
        if eng is None:
            eng = "act" if (scale is not None or ev_i[0] % 2 == 0) else "dve"
        if eng == "act":
            if scale is None:
                k.op("act", lambda h: h.activation(out=out_ap, in_=in_ap, func=AF.Copy), R, W)
            else:
                k.op("act", lambda h: h.activation(out=out_ap, in_=in_ap, func=AF.Copy, scale=scale), R, W)
        else:
            assert scale is None
            k.op(eng, lambda h: h.tensor_copy(out=out_ap, in_=in_ap), R, W)

    def load_w(dst, dram, nrows, ncols):
        for c in range(nrows // 128):
            k.dma("pool", dst.ap[:, c, 0:ncols], dram[c * 128:(c + 1) * 128, 0:ncols], dst, True)

    def rstd_of(pieces, R, P, small, n, col=0):
        np_ = len(pieces)
        for i, ap in enumerate(pieces):
            k.op("act", lambda h, ap=ap, i=i: h.activation(
                out=junk.ap[:P, 0:ap.shape[-1]], in_=ap, func=AF.Square,
                accum_out=small.ap[:P, 10 + i:11 + i]), list(R), [junk, small])
        if np_ > 1:
            k.op("dve", lambda h: h.tensor_reduce(out=small.ap[:P, 9:10], in_=small.ap[:P, 10:10 + np_],
                                                  axis=AX.X, op=ALU.add), [small], [small])
            src = small.ap[:P, 9:10]
        else:
            src = small.ap[:P, 10:11]
        k.op("act", lambda h: h.activation(out=small.ap[:P, 8:9], in_=src, func=AF.Ln, scale=1.0 / n, bias=epsb.ap[:P, 0:1]),
             [small, epsb], [small])
        k.op("act", lambda h: h.activation(out=small.ap[:P, col:col + 1], in_=small.ap[:P, 8:9], func=AF.Exp,
                                           scale=-0.5), [small], [small])

    def transpose_blocks(src_buf, src_ap_fn, nblk, P, dst_buf, dst_ap_fn, bw=128, R=()):
        b0 = 0
        while b0 < nblk:
            nb_ = min(8, nblk - b0)
            tb = nextT()
            fns = []
            for j in range(nb_):
                fns.append(lambda h, j=j, b0=b0, tb=tb: h.transpose(
                    tb.ap[:bw, j * 128:j * 128 + P], src_ap_fn(b0 + j), ident.ap[:P, :P]))
            k.ops("pe", fns, [src_buf, ident] + list(R), [tb])
            src3 = tb.ap[:bw, 0:nb_ * 128].rearrange("p (b c) -> p b c", c=128)[:, :, 0:P]
            evac(dst_ap_fn(b0, nb_), src3, [tb], [dst_buf])
            b0 += nb_

    def dense(lhs_buf, lhs_fn, nkc, w_buf, w_fn, P, ncols, consume, R=(), hold=False):
        g = 0
        res = []
        for c0 in range(0, ncols, 512):
            w = min(512, ncols - c0)
            ps = nextS()
            fns = []
            for c in range(nkc):
                fns.append(lambda h, c=c, c0=c0, w=w, ps=ps: h.matmul(
                    ps.ap[:P, 0:w], lhs_fn(c), w_fn(c, c0, w), start=(c == 0), stop=(c == nkc - 1)))
            k.ops("pe", fns, [lhs_buf, w_buf] + list(R), [ps])
            if hold:
                res.append((g, c0, w, ps, ps.ap[:P, 0:w]))
            else:
                consume(g, c0, w, ps, ps.ap[:P, 0:w])
            g += 1
        return res

    class UMaker:
        def __init__(self, scope, tag):
            self.xt = [alloc(scope, "xt%s%d" % (tag, i), [128, D], F32) for i in range(2)]
            self.xn = alloc(scope, "xn" + tag, [128, D], BF16)
            self.uT = [alloc(scope, "uT%s%d" % (tag, i), [128, KC, 128], BF16) for i in range(2)]
            self.sm = alloc(scope, "smu" + tag, [128, 16], F32)
            self.made = {}
            self.loaded = set()

        def get(self, t):
            self.load(t)
            if t not in self.made:
                self.made[t] = self.make(t)
            return self.made[t]

        def prefetch(self, t):
            if t < NT:
                self.get(t)

        def load(self, t):
            if t >= NT or t in self.loaded:
                return
            self.loaded.add(t)
            P = rows(t)
            k.dma("sp", self.xt[t % 2].ap[:P, :], x_all[tok0(t):tok0(t) + P, :], self.xt[t % 2], True)

        def make(self, t):
            P = rows(t)
            xb, u = self.xt[t % 2], self.uT[t % 2]
            rstd_of([xb.ap[:P, :]], [xb], P, self.sm, D)
            k.op("dve", lambda h: h.scalar_tensor_tensor(
                out=self.xn.ap[:P, :], in0=xb.ap[:P, :], scalar=self.sm.ap[:P, 0:1], in1=gpre_bc.ap[:P, :],
                op0=ALU.mult, op1=ALU.mult), [xb, self.sm, gpre_bc], [self.xn])
            transpose_blocks(self.xn, lambda b: self.xn.ap[:P, b * 128:(b + 1) * 128], 8, P, u,
                             lambda b0, n: u.ap[:, b0:b0 + n, 0:P])
            return u

    def rope(src, dst, cb, sb_, P, nh, half, tmp1, tmp2):
        W_ = nh * 2 * half
        sv = src.ap[:P, 0:W_].rearrange("p (h two f) -> p h two f", two=2, f=half)
        tv = tmp2.ap[:P, 0:W_].rearrange("p (h two f) -> p h two f", two=2, f=half)
        snv = sb_.ap[:P, 0:W_].rearrange("p (h two f) -> p h two f", two=2, f=half)
        k.op("pool", lambda h: h.tensor_tensor(out=tmp1.ap[:P, 0:W_], in0=src.ap[:P, 0:W_], in1=cb.ap[:P, 0:W_],
                                               op=ALU.mult), [src, cb], [tmp1])
        k.op("dve", lambda h: h.tensor_tensor(out=tv[:, :, 0, :], in0=sv[:, :, 1, :], in1=snv[:, :, 0, :],
                                              op=ALU.mult), [src, sb_], [tmp2])
        k.op("dve", lambda h: h.tensor_tensor(out=tv[:, :, 1, :], in0=sv[:, :, 0, :], in1=snv[:, :, 1, :],
                                              op=ALU.mult), [src, sb_], [tmp2])
        k.op("dve", lambda h: h.tensor_tensor(out=dst.ap[:P, 0:W_], in0=tmp1.ap[:P, 0:W_], in1=tmp2.ap[:P, 0:W_],
                                              op=ALU.add), [tmp1, tmp2], [dst])

    def gated_add(t, P, uTb, lhs_buf, lhs_fn, nkc, Wo, Wg, sgb, first=False):
        def consG2(g, c0, w, ps, ps_ap):
            k.op("act", lambda h: h.activation(out=sgb.ap[:P, c0:c0 + w], in_=ps_ap, func=AF.Sigmoid), [ps], [sgb])
        dense(uTb, lambda c: uTb.ap[:, c, 0:P], KC, Wg, lambda c, c0, w: Wg.ap[:, c, c0:c0 + w], P, D, consG2)

        def consA2(g, c0, w, ps, ps_ap):
            if first:
                k.op("dve", lambda h: h.tensor_tensor(out=mix[t].ap[:P, c0:c0 + w], in0=ps_ap,
                                                      in1=sgb.ap[:P, c0:c0 + w], op=ALU.mult), [ps, sgb], [mix[t]])
            else:
                k.op("dve", lambda h: h.tensor_tensor(out=sgb.ap[:P, c0:c0 + w], in0=ps_ap, in1=sgb.ap[:P, c0:c0 + w],
                                                      op=ALU.mult), [ps, sgb], [sgb])
                k.op("pool", lambda h: h.tensor_tensor(out=mix[t].ap[:P, c0:c0 + w], in0=mix[t].ap[:P, c0:c0 + w],
                                                       in1=sgb.ap[:P, c0:c0 + w], op=ALU.add), [sgb, mix[t]], [mix[t]])
        dense(lhs_buf, lhs_fn, nkc, Wo, lambda c, c0, w: Wo.ap[:, c, c0:c0 + w], P, D, consA2)

    gam = [1.0 - 2.0 ** (-5.0 - h) for h in range(RH)]

    esR = ExitStack()

    def sbR(name, shape, dt):
        return alloc(esR, name, shape, dt)

    WR = sbR("WR", [128, KC, 3072], BF16)
    Wro = sbR("Wro", [128, KC, D], BF16)
    WgA = sbR("WgA", [128, KC, D], BF16)
    um = UMaker(esR, "R")
    um.load(0)
    load_w(WR, w_in[:, 0:3072], D, 3072)
    load_w(Wro, w_ro, RH * DV, D)
    load_w(WgA, w_in[:, 7104 - 3072:7104 - 2048], D, D)
    dtc = sbR("dtc", [128, RH * 128], F32)
    qdc = sbR("qdc", [128, RH * 128], F32)
    kdc = sbR("kdc", [128, RH * DK], F32)
    bm = sbR("bm", [128, NB * PS], BF16)
    bmr = sbR("bmr", [128, NB], BF16)
    for (b_, d_) in ((dtc, c_dtp), (qdc, c_qdp), (kdc, c_kdp), (bm, c_bm), (bmr, c_bmr)):
        k.dma("sp", b_.ap[:, :], d_[:, :], b_, True)
    cosb = sbR("cosb", [128, 512], F32)
    sinb = sbR("sinb", [128, 512], F32)
    qf = sbR("qf", [128, 512], F32)
    kf = sbR("kf", [128, 512], F32)
    rt1 = sbR("rt1", [128, 512], F32)
    rt2 = sbR("rt2", [128, 512], F32)
    q_r = sbR("q_r", [128, 512], BF16)
    k_r = sbR("k_r", [128, 512], BF16)
    kd = sbR("kd", [128, 512], BF16)
    qd_r = sbR("qd_r", [128, 512], BF16)
    v_b = sbR("v_b", [128, 1024], BF16)
    sgr = sbR("sgr", [128, 1024], F32)
    qT = sbR("qT", [128, RH, 128], BF16)
    kT = sbR("kT", [128, RH, 128], BF16)
    qdT = sbR("qdT", [128, RH, 128], BF16)
    im = sbR("im", [128, 128], BF16)
    og = sbR("og", [128, 1024], BF16)
    ogT = sbR("ogT", [128, KC, 128], BF16)
    sga = sbR("sga", [128, 1024], F32)
    smR = sbR("smR", [128, 16], F32)
    s_f = sbR("s_f", [128, RH, DV], F32)
    s_b = sbR("s_b", [128, RH, DV], BF16)
    qdTm = sbR("qdTm", [128, NB, PS], BF16)
    kdm = sbR("kdm", [128, NB, DK], BF16)
    s0f = [sbR("s0f%d" % i, [128, DV], F32) for i in range(3)]
    s0b = [sbR("s0b%d" % i, [128, DV], BF16) for i in range(3)]
    snw = [sbR("snw%d" % i, [128, DV], F32) for i in range(2)]

    k.op("dve", lambda h: h.memset(s_f.ap[:, :, :], 0.0), [], [s_f])
    k.op("pool", lambda h: h.memset(s_b.ap[:, :, :], 0.0), [], [s_b])

    def _tileR(t, um):
        P = rows(t)
        smp = (t == TP)
        uTb = um.get(t)
        um.load(t + 1)
        if smp:
            for (b_, d_) in ((dtc, c_dts), (qdc, c_qds), (kdc, c_kds)):
                k.dma("sp", b_.ap[:, :], d_[:, :], b_, True)
        k.dma("sp", cosb.ap[:P, :], c_cosR[tok0(t):tok0(t) + P, :], cosb, True)
        k.dma("sp", sinb.ap[:P, :], c_sinR[tok0(t):tok0(t) + P, :], sinb, True)

        def consR(g, c0, w, ps, ps_ap, P=P):
            if g == 0:
                evac(qf.ap[:P, :], ps_ap, [ps], [qf])
            elif g == 1:
                evac(kf.ap[:P, :], ps_ap, [ps], [kf])
            elif g in (2, 3):
                evac(v_b.ap[:P, (g - 2) * 512:(g - 1) * 512], ps_ap, [ps], [v_b])
            else:
                k.op("act", lambda h: h.activation(out=sgr.ap[:P, (g - 4) * 512:(g - 3) * 512], in_=ps_ap,
                                                   func=AF.Silu), [ps], [sgr])

        dense(uTb, lambda c, uTb=uTb, P=P: uTb.ap[:, c, 0:P], KC, WR, lambda c, c0, w: WR.ap[:, c, c0:c0 + w],
              P, 3072, consR)
        um.prefetch(t + 1)
        rope(qf, q_r, cosb, sinb, P, RH, 64, rt1, rt2)
        rope(kf, k_r, cosb, sinb, P, RH, 64, rt1, rt2)
        k.op("pool", lambda h, P=P: h.tensor_tensor(out=kd.ap[:P, :], in0=k_r.ap[:P, :], in1=kdc.ap[:P, :],
                                                    op=ALU.mult), [k_r, kdc], [kd])
        k.op("pool", lambda h, P=P: h.tensor_tensor(out=qd_r.ap[:P, :], in0=q_r.ap[:P, :], in1=qdc.ap[:P, :],
                                                    op=ALU.mult), [q_r, qdc], [qd_r])
        tb = nextT()
        fns = []
        for j in range(4):
            fns.append(lambda h, j=j, tb=tb, P=P: h.transpose(tb.ap[:, j * 128:j * 128 + P],
                                                            q_r.ap[:P, j * 128:(j + 1) * 128], ident.ap[:P, :P]))
        for j in range(4):
            fns.append(lambda h, j=j, tb=tb, P=P: h.transpose(tb.ap[:, (4 + j) * 128:(4 + j) * 128 + P],
                                                            k_r.ap[:P, j * 128:(j + 1) * 128], ident.ap[:P, :P]))
        k.ops("pe", fns, [q_r, k_r, ident], [tb])
        tq = tb.ap[:, 0:512].rearrange("p (b c) -> p b c", c=128)[:, :, 0:P]
        tk = tb.ap[:, 512:1024].rearrange("p (b c) -> p b c", c=128)[:, :, 0:P]
        k.op("act", lambda h, tq=tq, P=P: h.activation(out=qT.ap[:, :, 0:P], in_=tq, func=AF.Copy), [tb], [qT])
        k.op("act", lambda h, tk=tk, P=P: h.activation(out=kT.ap[:, :, 0:P], in_=tk, func=AF.Copy), [tb], [kT])
        transpose_blocks(qd_r, lambda b, P=P: qd_r.ap[:P, b * 128:(b + 1) * 128], 4, P, qdT,
                         lambda b0, n, P=P: qdT.ap[:, b0:b0 + n, 0:P])
        for hd in range(RH):
            k.ops("pe", [lambda h, hd=hd, P=P: h.matmul(G_b.ap[:P, 0:P], kT.ap[:, hd, 0:P], qT.ap[:, hd, 0:P],
                                                      start=True, stop=True)], [kT, qT], [G_b])
            k.op("dve", lambda h, hd=hd, P=P: h.tensor_tensor(
                out=im.ap[:P, 0:P], in0=G_b.ap[:P, 0:P], in1=dtc.ap[:P, hd * 128:hd * 128 + P], op=ALU.mult),
                [G_b, dtc], [im])
            if not smp:
                fns = [lambda h, hd=hd, P=P: h.matmul(O_b.ap[:P, 0:DV], im.ap[:P, 0:P],
                                                    v_b.ap[:P, hd * DV:(hd + 1) * DV], start=True, stop=False),
                       lambda h, hd=hd, P=P: h.matmul(O_b.ap[:P, 0:DV], qdT.ap[:, hd, 0:P], s_b.ap[:, hd, :],
                                                    start=False, stop=True)]
                k.ops("pe", fns, [im, v_b, qdT, s_b], [O_b])
            else:
                k.op("pool", lambda h, hd=hd: h.tensor_tensor(
                    out=qdTm.ap[:, :, :],
                    in0=bm.ap[:, :].rearrange("p (b c) -> p b c", c=PS),
                    in1=qdT.ap[:, hd:hd + 1, 0:PS].to_broadcast([128, NB, PS]), op=ALU.mult), [bm, qdT], [qdTm])
                k.op("pool", lambda h, hd=hd: h.tensor_tensor(
                    out=kdm.ap[:PS, :, :],
                    in0=bmr.ap[:PS, :].unsqueeze(2).to_broadcast([PS, NB, DK]),
                    in1=kd.ap[:PS, hd * DK:(hd + 1) * DK].unsqueeze(1).to_broadcast([PS, NB, DK]), op=ALU.mult),
                    [bmr, kd], [kdm])
                k.ops("pe", [lambda h, hd=hd, P=P: h.matmul(O_b.ap[:P, 0:DV], im.ap[:P, 0:P],
                                                          v_b.ap[:P, hd * DV:(hd + 1) * DV], start=True, stop=False)],
                      [im, v_b], [O_b])
                for b in range(NB):
                    i3 = (hd * NB + b) % 3
                    r0 = (b * RH + hd) * DK
                    k.dma("sp", s0f[i3].ap[:, :], st_in[r0:r0 + DK, :], s0f[i3], True)
                    evac(s0b[i3].ap[:, :], s0f[i3].ap[:, :], [s0f[i3]], [s0b[i3]])
                    k.ops("pe", [lambda h, b=b, i3=i3, P=P: h.matmul(O_b.ap[:P, 0:DV], qdTm.ap[:, b, :],
                                                                   s0b[i3].ap[:, :], start=False,
                                                                   stop=(b == NB - 1))],
                          [qdTm, s0b[i3]], [O_b])
                    k.ops("pe", [lambda h, b=b, hd=hd: h.matmul(G_b.ap[:, 0:DV], kdm.ap[:PS, b, :],
                                                               v_b.ap[:PS, hd * DV:(hd + 1) * DV],
                                                               start=True, stop=True)], [kdm, v_b], [G_b])
                    sw = snw[b % 2]
                    k.op("dve", lambda h, hd=hd, i3=i3, sw=sw: h.scalar_tensor_tensor(
                        out=sw.ap[:, :], in0=s0f[i3].ap[:, :], scalar=float(gam[hd] ** DEC_SEQ), in1=G_b.ap[:, 0:DV],
                        op0=ALU.mult, op1=ALU.add), [s0f[i3], G_b], [sw])
                    k.dma("pool", o_rets[r0:r0 + DK, :], sw.ap[:, :], sw, False)
            rstd_of([O_b.ap[:P, 0:DV]], [O_b], P, smR, DV, col=hd)
            k.op("dve", lambda h, hd=hd, P=P: h.scalar_tensor_tensor(
                out=og.ap[:P, hd * DV:(hd + 1) * DV], in0=O_b.ap[:P, 0:DV], scalar=smR.ap[:P, hd:hd + 1],
                in1=sgr.ap[:P, hd * DV:(hd + 1) * DV], op0=ALU.mult, op1=ALU.mult), [O_b, smR, sgr], [og])
            if not smp:
                k.ops("pe", [lambda h, hd=hd, P=P: h.matmul(G_b.ap[:, 0:DV], kd.ap[:P, hd * DK:(hd + 1) * DK],
                                                          v_b.ap[:P, hd * DV:(hd + 1) * DV], start=True, stop=True)],
                      [kd, v_b], [G_b])
                k.op("dve", lambda h, hd=hd: h.scalar_tensor_tensor(
                    out=s_f.ap[:, hd, :], in0=s_f.ap[:, hd, :], scalar=float(gam[hd] ** 128), in1=G_b.ap[:, 0:DV],
                    op0=ALU.mult, op1=ALU.add), [s_f, G_b], [s_f])
                k.op("act", lambda h, hd=hd: h.activation(out=s_b.ap[:, hd, :], in_=s_f.ap[:, hd, :], func=AF.Copy),
                     [s_f], [s_b])
        if t == TP - 1:
            k.dma("pool", o_retp.rearrange("(h d) e -> d h e", d=DK), s_f.ap[:, :, :], s_f, False)
        transpose_blocks(og, lambda b, P=P: og.ap[:P, b * 128:(b + 1) * 128], 8, P, ogT,
                         lambda b0, n, P=P: ogT.ap[:, b0:b0 + n, 0:P])
        gated_add(t, P, uTb, ogT, lambda c, P=P: ogT.ap[:, c, 0:P], KC, Wro, WgA, sga, first=True)
    for t in range(NT):
        _tileR(t, um)
    k.barrier()
    esR.close()

    esM = ExitStack()

    def sbM(name, shape, dt):
        return alloc(esM, name, shape, dt)

    WM = sbM("WM", [128, KC, 704], BF16)
    Wuq = sbM("Wuq", [128, 3, 1536], BF16)
    wukT = sbM("wukT", [128, MH, KVL], BF16)
    Wuv = sbM("Wuv", [128, MH * 2, DVH], BF16)
    Wmo = sbM("Wmo", [128, MH, D], BF16)
    WgB = sbM("WgB", [128, KC, D], BF16)
    um = UMaker(esM, "M")
    um.load(0)
    load_w(WM, w_in[:, 3072:3776], D, 704)
    uqv = w_uq.rearrange("k (h c) -> k h c", c=DN + DR)
    gql_bc = sbM("gql_bc", [128, QL], F32)
    k.dma("sp", gql_bc.ap[:, :], g_ql[0].partition_broadcast(128), gql_bc, True)
    for c in range(3):
        for (lo, hi, off) in ((0, DN, 0), (DN, DN + DR, MH * DN)):
            wpart = hi - lo
            k.dma("pool", Wuq.ap[:, c, off:off + MH * wpart].rearrange("p (h c) -> p h c", c=wpart),
                  uqv[c * 128:(c + 1) * 128, :, lo:hi], Wuq, True)
    ol = sbM("ol", [128, MH * 2, 128], BF16)
    load_w(ol, w_uk, MH * KVL, DN)
    transpose_blocks(ol, lambda b: ol.ap[:, b, :], MH * 2, 128, wukT,
                     lambda b0, n: wukT.ap[:, :, :].rearrange("p h (lc c) -> p (h lc) c", c=128)[:, b0:b0 + n, :])
    load_w(Wuv, w_uv, MH * KVL, DVH)
    load_w(Wmo, w_mo, MH * DVH, D)
    load_w(WgB, w_in[:, 7104 - 2048:7104 - 1024], D, D)
    gkv = sbM("gkv", [128, KVL], F32)
    k.dma("sp", gkv.ap[:, :], g_kv[0].partition_broadcast(128), gkv, True)
    cosm = sbM("cosm", [128, 512], F32)
    sinm = sbM("sinm", [128, 512], F32)
    smM = sbM("smM", [128, 16], F32)
    cqn = sbM("cqn", [128, QL], BF16)
    cqT = sbM("cqT", [128, 3, 128], BF16)
    ckf = [sbM("ckf%d" % i, [128, KVL], F32) for i in range(2)]
    kpf = sbM("kpf", [128, DR], F32)
    kpr = [sbM("kpr%d" % i, [128, DR], F32) for i in range(2)]
    kp2 = sbM("kp2", [128, DR], BF16)
    mt1 = sbM("mt1", [128, 512], F32)
    mt2 = sbM("mt2", [128, 512], F32)
    qn_b = sbM("qn_b", [128, MH * DN], BF16)
    qpf = sbM("qpf", [128, 512], F32)
    qp_b = sbM("qp_b", [128, 512], BF16)
    qnT = sbM("qnT", [128, MH, 128], BF16)
    qpT = sbM("qpT", [64, MH, 128], BF16)
    qlT = sbM("qlT", [128, MH * 2, 128], BF16)
    olT = sbM("olT", [128, MH * 2, 128], BF16)
    omT = sbM("omT", [128, MH, 128], BF16)
    sgb = sbM("sgb", [128, D], F32)

    def mla_front(t, uTb, ckvT_ap, kpeT_ap, V_ap, Vbuf, ckvT, kpeT):
        P = rows(t)
        cb, sn = cosm, sinm
        k.dma("sp", cb.ap[:P, :], c_cosM[tok0(t):tok0(t) + P, :], cb, True)
        k.dma("sp", sn.ap[:P, :], c_sinM[tok0(t):tok0(t) + P, :], sn, True)
        ck, kr = ckf[t % 2], kpr[t % 2]

        def consQ(g, c0, w, ps, ps_ap):
            rstd_of([ps_ap], [ps], P, smM, QL, col=0)
            k.op("dve", lambda h: h.scalar_tensor_tensor(
                out=cqn.ap[:P, :], in0=ps_ap, scalar=smM.ap[:P, 0:1], in1=gql_bc.ap[:P, :],
                op0=ALU.mult, op1=ALU.mult), [ps, smM, gql_bc], [cqn])
        dense(uTb, lambda c: uTb.ap[:, c, 0:P], KC, WM, lambda c, c0, w: WM.ap[:, c, c0:c0 + w], P, QL, consQ)

        def consK(g, c0, w, ps, ps_ap):
            rstd_of([ps.ap[:P, 0:KVL]], [ps], P, smM, KVL, col=1)
            k.op("dve", lambda h: h.scalar_tensor_tensor(
                out=ck.ap[:P, :], in0=ps.ap[:P, 0:KVL], scalar=smM.ap[:P, 1:2], in1=gkv.ap[:P, :],
                op0=ALU.mult, op1=ALU.mult), [ps, smM, gkv], [ck])
            k.op("act", lambda h: h.activation(out=kpf.ap[:P, :], in_=ps.ap[:P, KVL:KVL + DR], func=AF.Copy),
                 [ps], [kpf])
        dense(uTb, lambda c: uTb.ap[:, c, 0:P], KC, WM, lambda c, c0, w: WM.ap[:, c, QL + c0:QL + c0 + w], P,
              KVL + DR, consK)
        k.dma("pool", o_ckv[tok0(t):tok0(t) + P, :], ck.ap[:P, :], ck, False)
        k.op("act", lambda h: h.activation(out=V_ap, in_=ck.ap[:P, :], func=AF.Copy), [ck], [Vbuf])
        transpose_blocks(Vbuf, lambda b: V_ap[:, b * 128:(b + 1) * 128], 2, P, ckvT, lambda b0, n: ckvT_ap[:, b0:b0 + n, :])
        kv = kpf.ap[:P, :].rearrange("p (two f) -> p two f", two=2)
        tv = mt2.ap[:P, 0:DR].rearrange("p (two f) -> p two f", two=2)
        sv = sn.ap[:P, 0:DR].rearrange("p (two f) -> p two f", two=2)
        k.op("pool", lambda h: h.tensor_tensor(out=mt1.ap[:P, 0:DR], in0=kpf.ap[:P, :], in1=cb.ap[:P, 0:DR],
                                               op=ALU.mult), [kpf, cb], [mt1])
        k.op("dve", lambda h: h.tensor_tensor(out=tv[:, 0, :], in0=kv[:, 1, :], in1=sv[:, 0, :], op=ALU.mult),
             [kpf, sn], [mt2])
        k.op("dve", lambda h: h.tensor_tensor(out=tv[:, 1, :], in0=kv[:, 0, :], in1=sv[:, 1, :], op=ALU.mult),
             [kpf, sn], [mt2])
        k.op("dve", lambda h: h.tensor_tensor(out=kr.ap[:P, :], in0=mt1.ap[:P, 0:DR], in1=mt2.ap[:P, 0:DR],
                                              op=ALU.add), [mt1, mt2], [kr])
        k.dma("pool", o_kpe[tok0(t):tok0(t) + P, :], kr.ap[:P, :], kr, False)
        k.op("act", lambda h: h.activation(out=kp2.ap[:P, 0:DR], in_=kr.ap[:P, :], func=AF.Copy), [kr], [kp2])
        transpose_blocks(kp2, lambda b: kp2.ap[:P, :], 1, P, kpeT, lambda b0, n: kpeT_ap.unsqueeze(1), bw=DR)
        transpose_blocks(cqn, lambda b: cqn.ap[:P, b * 128:(b + 1) * 128], 3, P, cqT,
                         lambda b0, n: cqT.ap[:, b0:b0 + n, 0:P])

        def consUQ(g, c0, w, ps, ps_ap):
            if g < 2:
                evac(qn_b.ap[:P, c0:c0 + w], ps_ap, [ps], [qn_b])
            else:
                evac(qpf.ap[:P, :], ps_ap, [ps], [qpf])
        dense(cqT, lambda c: cqT.ap[:, c, 0:P], 3, Wuq, lambda c, c0, w: Wuq.ap[:, c, c0:c0 + w], P, 1536, consUQ)
        rope(qpf, qp_b, cb, sn, P, MH, 32, mt1, mt2)
        transpose_blocks(qp_b, lambda b: qp_b.ap[:P, b * DR:(b + 1) * DR], MH, P, qpT,
                         lambda b0, n: qpT.ap[:, b0:b0 + n, 0:P], bw=DR)
        transpose_blocks(qn_b, lambda b: qn_b.ap[:P, b * 128:(b + 1) * 128], 8, P, qnT,
                         lambda b0, n: qnT.ap[:, b0:b0 + n, 0:P])
        for q4 in range(4):
            ps = nextS()
            for j in range(4):
                hl = q4 * 4 + j
                hd, lc = hl // 2, hl % 2
                k.ops("pe", [lambda h, hd=hd, lc=lc, ps=ps, j=j: h.matmul(
                    ps.ap[:, j * 128:j * 128 + P], wukT.ap[:, hd, lc * 128:(lc + 1) * 128],
                    qnT.ap[:, hd, 0:P], start=True, stop=True)], [wukT, qnT], [ps])
            src = ps.ap[:, :].rearrange("p (b c) -> p b c", c=128)[:, :, 0:P]
            evac(qlT.ap[:, q4 * 4:q4 * 4 + 4, 0:P], src, [ps], [qlT])

    def mla_tail(t, uTb):
        P = rows(t)
        for half in range(2):
            ps = nextS()
            for j in range(4):
                hd = half * 4 + j
                fns = [lambda h, hd=hd, lc=lc, j=j, ps=ps: h.matmul(
                    ps.ap[:, j * 128:j * 128 + P], Wuv.ap[:, hd * 2 + lc, :], olT.ap[:, hd * 2 + lc, 0:P],
                    start=(lc == 0), stop=(lc == 1)) for lc in range(2)]
                k.ops("pe", fns, [Wuv, olT], [ps])
            src = ps.ap[:, :].rearrange("p (b c) -> p b c", c=128)[:, :, 0:P]
            evac(omT.ap[:, half * 4:half * 4 + 4, 0:P], src, [ps], [omT])
        gated_add(t, P, uTb, omT, lambda c: omT.ap[:, c, 0:P], MH, Wmo, WgB, sgb)

    esMp = ExitStack()
    ckvT = alloc(esMp, "ckvT", [128, 2, TP * 128], BF16)
    kpeT = alloc(esMp, "kpeT", [DR, TP * 128], BF16)
    Vc = alloc(esMp, "Vc", [128, TP, KVL + 16], BF16)
    caus = alloc(esMp, "caus", [128, 128], BF16)
    k.dma("sp", caus.ap[:, :], c_caus[:, :], caus, True)
    k.op("pool", lambda h: h.memset(Vc.ap[:, :, :], 1.0), [], [Vc])
    p_bs = [alloc(esMp, "p_b%d" % i, [128, 2048], BF16) for i in range(2)]
    smS = [alloc(esMp, "smS%d" % i, [128, 8], F32) for i in range(2)]
    pT = alloc(esMp, "pT", [128, 16, 128], BF16)
    def _tileMp(t, um):
        P = 128
        uTb = um.get(t)
        um.load(t + 1)
        mla_front(t, uTb, ckvT.ap[:, :, tok0(t):tok0(t) + P], kpeT.ap[:, tok0(t):tok0(t) + P], Vc.ap[:, t, 0:KVL], Vc,
                  ckvT, kpeT)
        um.prefetch(t + 1)
        nk = (t + 1) * 128
        def scores(hd):
            for g0 in range(0, nk, 512):
                w = min(512, nk - g0)
                ps = S[g0 // 512]
                fns = [
                    lambda h, g0=g0, w=w, ps=ps, hd=hd: h.matmul(ps.ap[:P, 0:w], qlT.ap[:, hd * 2, 0:P],
                                                               ckvT.ap[:, 0, g0:g0 + w], start=True, stop=False),
                    lambda h, g0=g0, w=w, ps=ps, hd=hd: h.matmul(ps.ap[:P, 0:w], qlT.ap[:, hd * 2 + 1, 0:P],
                                                               ckvT.ap[:, 1, g0:g0 + w], start=False, stop=False),
                    lambda h, g0=g0, w=w, ps=ps, hd=hd: h.matmul(
                        ps.ap[:P, 0:w], qpT.ap[:, hd, 0:P], kpeT.ap[:, g0:g0 + w],
                        start=False, stop=True)]
                k.ops("pe", fns, [qlT, ckvT, qpT, kpeT], [ps])

        nsb = (nk + 511) // 512
        sc_ap = S_t[:P, 0:nk]

        def soft(hd):
            p_b, sm = p_bs[hd % 2], smS[hd % 2]
            k.op("dve", lambda h: h.tensor_reduce(out=sm.ap[:P, 2:3], in_=sc_ap, axis=AX.X, op=ALU.max),
                 S[0:nsb], [sm])
            k.op("dve", lambda h: h.tensor_scalar(out=sm.ap[:P, 3:4], in0=sm.ap[:P, 2:3], scalar1=-SCALE,
                                                  scalar2=None, op0=ALU.mult), [sm], [sm])
            k.op("act", lambda h: h.activation(out=p_b.ap[:P, 0:nk], in_=sc_ap, func=AF.Exp,
                                               bias=sm.ap[:P, 3:4], scale=SCALE), S[0:nsb] + [sm], [p_b])
            k.op("pool", lambda h: h.tensor_tensor(out=p_b.ap[:P, nk - 128:nk], in0=p_b.ap[:P, nk - 128:nk],
                                                   in1=caus.ap[:, :], op=ALU.mult), [p_b, caus], [p_b])

        def pv(hd):
            p_b, sm = p_bs[hd % 2], smS[hd % 2]
            transpose_blocks(p_b, lambda b: p_b.ap[:P, b * 128:(b + 1) * 128], t + 1, P, pT,
                             lambda b0, n: pT.ap[:, b0:b0 + n, 0:P])
            fns = [lambda h, kb=kb: h.matmul(O_b.ap[:P, 0:KVL + 1], pT.ap[:, kb, 0:P], Vc.ap[:, kb, 0:KVL + 1],
                                            start=(kb == 0), stop=(kb == t)) for kb in range(t + 1)]
            k.ops("pe", fns, [pT, Vc], [O_b])
            k.op("dve", lambda h: h.reciprocal(out=sm.ap[:P, 5:6], in_=O_b.ap[:P, KVL:KVL + 1]), [O_b], [sm])
            k.op("act", lambda h: h.activation(
                out=ol.ap[:P, hd * 2:hd * 2 + 2, :], in_=O_b.ap[:P, 0:KVL].rearrange("p (lc c) -> p lc c", c=128),
                func=AF.Copy, scale=sm.ap[:P, 5:6]), [O_b, sm], [ol])

        scores(0)
        soft(0)
        for hd in range(MH):
            if hd + 1 < MH:
                scores(hd + 1)
                soft(hd + 1)
            pv(hd)
        transpose_blocks(ol, lambda b: ol.ap[:P, b, :], 16, P, olT, lambda b0, n: olT.ap[:, b0:b0 + n, 0:P])
        mla_tail(t, uTb)
    for t in range(TP):
        _tileMp(t, um)
    k.barrier()
    esMp.close()

    esMs = ExitStack()
    t = TP
    P = PS
    ckvTs = alloc(esMs, "ckvTs", [128, 2, PS], BF16)
    kpeTs = alloc(esMs, "kpeTs", [DR, PS], BF16)
    Vs = alloc(esMs, "Vs", [128, KVL], BF16)
    qlb = alloc(esMs, "qlb", [128, 2, NB, 32], BF16)
    qpb = alloc(esMs, "qpb", [64, NB, 32], BF16)
    big = alloc(esMs, "big", [32, 128], F32)
    pidx = alloc(esMs, "pidx", [128, NB * (NPG // 8)], I32)
    goff = alloc(esMs, "goff", [128, NB * (NPG // 8)], I32)
    gidx = alloc(esMs, "gidx", [128, NB * (NPG // 8)], I32)
    PGC = 8
    assert NPG % 8 == 0
    NCH = NPG // PGC
    assert NCH * PGC == NPG and NCH + 1 <= 9
    QG = (PGC + 3) // 4
    kbc = [alloc(esMs, "kbc%d" % i, [128, PGC, KVL], BF16) for i in range(3)]
    kbp = [alloc(esMs, "kbp%d" % i, [128, PGC, DR], BF16) for i in range(3)]
    KTc = [alloc(esMs, "KTc%d" % i, [128, 3, PGC * 128], BF16) for i in range(2)]
    ps_sb = alloc(esMs, "ps_sb", [32, PGC * 128], BF16)
    pTs = alloc(esMs, "pTs", [128, PGC, 32], BF16)
    oc = alloc(esMs, "oc", [32, 9, KVL], F32)
    mc = alloc(esMs, "mc", [32, 48], F32)
    ob = alloc(esMs, "ob", [32, KVL], F32)
    obb = alloc(esMs, "obb", [32, KVL], BF16)
    snew = alloc(esMs, "snew", [32, PS], F32)
    k.dma("sp", big.ap[:, :], c_big[0:32, :], big, True)
    NBC = NB * (NPG // 8)
    ptv = ptab.rearrange("o (bc j) -> j (o bc)", j=8)
    for j in range(8):
        k.dma("sp", pidx.ap[j * 16:(j + 1) * 16, :], ptv[j].partition_broadcast(16), pidx, True)
    k.dma("sp", goff.ap[:, :], c_goff[:, :], goff, True)
    k.op("dve", lambda h: h.tensor_single_scalar(out=gidx.ap[:, :], in_=pidx.ap[:, :], scalar=4,
                                                 op=ALU.logical_shift_left), [pidx], [gidx])
    k.op("dve", lambda h: h.tensor_tensor(out=gidx.ap[:, :], in0=gidx.ap[:, :], in1=goff.ap[:, :], op=ALU.bitwise_or),
         [gidx, goff], [gidx])
    ckv_rows = c_ckv.rearrange("n (g r) d -> (n g) (r d)", r=8)
    kpe_rows = c_kpe.rearrange("n (g r) d -> (n g) (r d)", r=8)
    uTb = um.get(t)
    mla_front(t, uTb, ckvTs.ap[:, :, :], kpeTs.ap[:, :], Vs.ap[:P, :], Vs, ckvTs, kpeTs)
    for lc in range(2):
        k.op("pool", lambda h, lc=lc: h.tensor_copy(
            out=qlb.ap[:, lc, :, :].rearrange("p b (h i) -> p b h i", i=4),
            in_=qlT.ap[:, :, 0:PS].rearrange("p (h lc) (b i) -> p lc b h i", lc=2, i=4)[:, lc]), [qlT], [qlb])
    k.op("pool", lambda h: h.tensor_copy(
        out=qpb.ap[:, :, :].rearrange("p b (h i) -> p b h i", i=4),
        in_=qpT.ap[:, :, 0:PS].rearrange("p h (b i) -> p b h i", i=4)), [qpT], [qpb])
    seq = [(b, c) for b in range(NB) for c in range(NCH)]

    def phaseA(i):
        b, c = seq[i]
        kc_, kp_ = kbc[i % 3], kbp[i % 3]
        par = i % 2
        KT = KTc[par]
        bc = b * NCH + c
        k.dma("pool", None, None, kc_, True, R=[gidx], fn=lambda h, bc=bc, kc_=kc_: h.indirect_dma_start(
            out=kc_.ap[:, :, :].rearrange("p a d -> p (a d)"), out_offset=None, in_=ckv_rows,
            in_offset=bass.IndirectOffsetOnAxis(ap=gidx.ap[:, bc:bc + 1], axis=0)))
        k.dma("pool", None, None, kp_, True, R=[gidx], fn=lambda h, bc=bc, kp_=kp_: h.indirect_dma_start(
            out=kp_.ap[:, :, :].rearrange("p a d -> p (a d)"), out_offset=None, in_=kpe_rows,
            in_offset=bass.IndirectOffsetOnAxis(ap=gidx.ap[:, bc:bc + 1], axis=0)))
        for q4 in range(QG):
            npg4 = min(4, PGC - q4 * 4)
            for blk in range(3):
                bw = 128 if blk < 2 else 64
                tb = nextT()
                fns = []
                for j in range(npg4):
                    pg = q4 * 4 + j
                    src_ = kc_.ap[:, pg, blk * 128:(blk + 1) * 128] if blk < 2 else kp_.ap[:, pg, :]
                    fns.append(lambda h, j=j, bw=bw, tb=tb, src_=src_: h.transpose(
                        tb.ap[:bw, j * 128:(j + 1) * 128], src_,
                        ident.ap[:, :]))
                k.ops("pe", fns, [kc_, kp_, ident], [tb])
                evac(KT.ap[:bw, blk, q4 * 512:q4 * 512 + npg4 * 128], tb.ap[:bw, 0:npg4 * 128], [tb], [KT])
        nkc = PGC * 128
        for g0 in range(0, nkc, 512):
            w = min(512, nkc - g0)
            ps = S[par * 2 + g0 // 512]
            fns = [
                lambda h, g0=g0, w=w, ps=ps, b=b, KT=KT: h.matmul(ps.ap[:32, 0:w], qlb.ap[:, 0, b, :],
                                                         KT.ap[:, 0, g0:g0 + w], start=True, stop=False),
                lambda h, g0=g0, w=w, ps=ps, b=b, KT=KT: h.matmul(ps.ap[:32, 0:w], qlb.ap[:, 1, b, :],
                                                         KT.ap[:, 1, g0:g0 + w], start=False, stop=False),
                lambda h, g0=g0, w=w, ps=ps, b=b, KT=KT: h.matmul(ps.ap[:32, 0:w], qpb.ap[:, b, :],
                                                         KT.ap[0:64, 2, g0:g0 + w], start=False, stop=True)]
            k.ops("pe", fns, [qlb, qpb, KT], [ps])

    def phaseB(i):
        b, c = seq[i]
        kc_ = kbc[i % 3]
        par = i % 2
        nkc = PGC * 128
        sc_ap = S_t[:32, par * 1024:par * 1024 + nkc]
        scR = [S[par * 2], S[par * 2 + 1]]
        k.op("dve", lambda h, c=c, sc_ap=sc_ap: h.tensor_reduce(out=mc.ap[:, c:c + 1], in_=sc_ap,
                                                               axis=AX.X, op=ALU.max), scR, [mc])
        k.op("dve", lambda h, c=c: h.tensor_scalar(out=mc.ap[:, 20 + c:21 + c], in0=mc.ap[:, c:c + 1],
                                                  scalar1=-SCALE, scalar2=None, op0=ALU.mult), [mc], [mc])
        k.op("act", lambda h, c=c, nkc=nkc, sc_ap=sc_ap: h.activation(
            out=ps_sb.ap[:, 0:nkc], in_=sc_ap, func=AF.Exp, bias=mc.ap[:, 20 + c:21 + c], scale=SCALE,
            accum_out=mc.ap[:, 10 + c:11 + c]), scR + [mc], [ps_sb, mc])

    def pvB(i):
        b, c = seq[i]
        kc_ = kbc[i % 3]
        transpose_blocks(ps_sb, lambda bb: ps_sb.ap[:, bb * 128:(bb + 1) * 128], PGC, 32, pTs,
                         lambda b0, n: pTs.ap[:, b0:b0 + n, :])
        fns = [lambda h, pg=pg, kc_=kc_: h.matmul(O_b.ap[:32, 0:KVL], pTs.ap[:, pg, :], kc_.ap[:, pg, :],
                                                 start=(pg == 0), stop=(pg == PGC - 1)) for pg in range(PGC)]
        k.ops("pe", fns, [pTs, kc_], [O_b])
        evac(oc.ap[:, c, :], O_b.ap[:32, 0:KVL], [O_b], [oc])

    def tailB(b):
        c = NCH
        fns = [
            lambda h, b=b: h.matmul(G_b.ap[:32, 0:PS], qlb.ap[:, 0, b, :], ckvTs.ap[:, 0, :], start=True, stop=False),
            lambda h, b=b: h.matmul(G_b.ap[:32, 0:PS], qlb.ap[:, 1, b, :], ckvTs.ap[:, 1, :], start=False, stop=False),
            lambda h, b=b: h.matmul(G_b.ap[:32, 0:PS], qpb.ap[:, b, :], kpeTs.ap[:, :], start=False, stop=True)]
        k.ops("pe", fns, [qlb, qpb, ckvTs, kpeTs], [G_b])
        k.op("dve", lambda h, b=b: h.tensor_tensor(out=snew.ap[:, :], in0=G_b.ap[:32, 0:PS],
                                                   in1=big.ap[:, 64 - 4 * b:128 - 4 * b], op=ALU.add),
             [G_b, big], [snew])
        k.op("dve", lambda h, c=c: h.tensor_reduce(out=mc.ap[:, c:c + 1], in_=snew.ap[:, :], axis=AX.X, op=ALU.max),
             [snew], [mc])
        k.op("dve", lambda h, c=c: h.tensor_scalar(out=mc.ap[:, 20 + c:21 + c], in0=mc.ap[:, c:c + 1], scalar1=-SCALE,
                                                  scalar2=None, op0=ALU.mult), [mc], [mc])
        k.op("act", lambda h, c=c: h.activation(out=ps_sb.ap[:, 0:PS], in_=snew.ap[:, :], func=AF.Exp,
                                               bias=mc.ap[:, 20 + c:21 + c], scale=SCALE,
                                               accum_out=mc.ap[:, 10 + c:11 + c]), [snew, mc], [ps_sb, mc])
        transpose_blocks(ps_sb, lambda bb: ps_sb.ap[:, 0:PS], 1, 32, pTs, lambda b0, n: pTs.ap[:PS, 0:1, :], bw=PS)
        k.ops("pe", [lambda h: h.matmul(O_b.ap[:32, 0:KVL], pTs.ap[:PS, 0, :], Vs.ap[:PS, :], start=True, stop=True)],
              [pTs, Vs], [O_b])
        evac(oc.ap[:, c, :], O_b.ap[:32, 0:KVL], [O_b], [oc])
        nch = NCH + 1
        k.op("dve", lambda h: h.tensor_reduce(out=mc.ap[:, 30:31], in_=mc.ap[:, 0:nch], axis=AX.X, op=ALU.max),
             [mc], [mc])
        k.op("dve", lambda h: h.tensor_scalar(out=mc.ap[:, 31:32], in0=mc.ap[:, 30:31], scalar1=-SCALE, scalar2=None,
                                              op0=ALU.mult), [mc], [mc])
        k.op("act", lambda h: h.activation(out=mc.ap[:, 20:20 + nch], in_=mc.ap[:, 0:nch], func=AF.Exp,
                                           bias=mc.ap[:, 31:32], scale=SCALE), [mc], [mc])
        k.op("dve", lambda h: h.tensor_tensor(out=mc.ap[:, 34:34 + nch], in0=mc.ap[:, 20:20 + nch],
                                              in1=mc.ap[:, 10:10 + nch], op=ALU.mult), [mc], [mc])
        k.op("dve", lambda h: h.tensor_reduce(out=mc.ap[:, 32:33], in_=mc.ap[:, 34:34 + nch], axis=AX.X, op=ALU.add),
             [mc], [mc])
        k.op("dve", lambda h: h.reciprocal(out=mc.ap[:, 33:34], in_=mc.ap[:, 32:33]), [mc], [mc])
        k.op("dve", lambda h: h.tensor_scalar(out=mc.ap[:, 20:20 + nch], in0=mc.ap[:, 20:20 + nch],
                                              scalar1=mc.ap[:, 33:34], scalar2=None, op0=ALU.mult), [mc], [mc])
        k.op("dve", lambda h: h.tensor_scalar(out=ob.ap[:, :], in0=oc.ap[:, 0, :], scalar1=mc.ap[:, 20:21],
                                              scalar2=None, op0=ALU.mult), [oc, mc], [ob])
        for c2 in range(1, nch):
            k.op("dve", lambda h, c2=c2: h.scalar_tensor_tensor(
                out=ob.ap[:, :], in0=oc.ap[:, c2, :], scalar=mc.ap[:, 20 + c2:21 + c2], in1=ob.ap[:, :],
                op0=ALU.mult, op1=ALU.add), [oc, mc, ob], [ob])
        k.op("act", lambda h: h.activation(out=obb.ap[:, :], in_=ob.ap[:, :], func=AF.Copy), [ob], [obb])
        tb = nextT()
        fns = [lambda h, lc=lc, tb=tb: h.transpose(tb.ap[:, lc * 128:lc * 128 + 32], obb.ap[:, lc * 128:(lc + 1) * 128],
                                                  ident.ap[:32, :32]) for lc in range(2)]
        k.ops("pe", fns, [obb, ident], [tb])
        for lc in range(2):
            dst = olT.ap[:, :, 4 * b:4 * b + 4].rearrange("p (h lc) i -> p h lc i", lc=2)[:, :, lc, :]
            src = tb.ap[:, lc * 128:lc * 128 + 32].rearrange("p (h i) -> p h i", i=4)
            evac(dst, src, [tb], [olT])

    phaseA(0)
    for i, (b, c) in enumerate(seq):
        phaseB(i)
        if i + 1 < len(seq):
            phaseA(i + 1)
        pvB(i)
        if c == NCH - 1:
            tailB(b)
    mla_tail(TP, uTb)
    k.barrier()
    esMs.close()
    esM.close()

    esX = ExitStack()

    def sbX(name, shape, dt):
        return alloc(esX, name, shape, dt)

    Wxq = sbX("Wxq", [128, KC, XH * XD], BF16)
    Wmk = sbX("Wmk", [128, KC, XH * XD], BF16)
    Wmv = sbX("Wmv", [128, KC, XH * XD], BF16)
    Wxo = sbX("Wxo", [128, 2, D], BF16)
    WgC = sbX("WgC", [128, KC, D], BF16)
    um = UMaker(esX, "X")
    um.load(0)
    load_w(Wxq, w_in[:, 3776:4032], D, 256)
    load_w(Wmk, w_mk, D, 256)
    load_w(Wmv, w_mv, D, 256)
    load_w(Wxo, w_xo, XH * XD, D)
    load_w(WgC, w_in[:, 7104 - 1024:7104], D, D)
    mkT = [sbX("mkT%d" % i, [XD, XH, NMEM], BF16) for i in range(NB + 1)]
    mvb = [sbX("mvb%d" % i, [128, 2, XH * XD], BF16) for i in range(NB + 1)]
    mst = [sbX("mst%d" % i, [128, D], F32) for i in range(2)]
    mnb = sbX("mnb", [128, D], BF16)
    mnT = sbX("mnT", [128, KC, NMEM], BF16)
    mkf = [sbX("mkf%d" % i, [128, XH * XD], F32) for i in range(2)]
    mkb = sbX("mkb", [128, 2, XH * XD], BF16)
    smX = sbX("smX", [128, 16], F32)
    gmem_bc = sbX("gmem_bc", [128, D], F32)
    k.dma("sp", gmem_bc.ap[:, :], g_mem[0].partition_broadcast(128), gmem_bc, True)
    bmx = sbX("bmx", [128, NB * PS], BF16)
    k.dma("sp", bmx.ap[:, :], c_bm[:, :], bmx, True)

    def make_mkT(slot):
        tb = nextT()
        fns = []
        for mt in range(2):
            for hh in range(XH):
                j = mt * XH + hh
                fns.append(lambda h, mt=mt, hh=hh, j=j, tb=tb: h.transpose(
                    tb.ap[:XD, j * 128:(j + 1) * 128], mkb.ap[:, mt, hh * XD:(hh + 1) * XD], ident.ap[:, :]))
        k.ops("pe", fns, [mkb, ident], [tb])
        evac(mkT[slot].ap[:, :, :].rearrange("p hh (mt c) -> p mt hh c", c=128),
             tb.ap[:XD, 0:1024].rearrange("p (mt hh c) -> p mt hh c", mt=2, hh=XH), [tb], [mkT[slot]])

    for mt in range(2):
        ms = mst[mt]
        k.dma("sp", ms.ap[:, :], mem_p[mt * 128:(mt + 1) * 128, :], ms, True)
        rstd_of([ms.ap[:, :]], [ms], 128, smX, D)
        k.op("dve", lambda h, ms=ms: h.scalar_tensor_tensor(
            out=mnb.ap[:, :], in0=ms.ap[:, :], scalar=smX.ap[:, 0:1], in1=gmem_bc.ap[:, :],
            op0=ALU.mult, op1=ALU.mult), [ms, smX, gmem_bc], [mnb])
        transpose_blocks(mnb, lambda b: mnb.ap[:, b * 128:(b + 1) * 128], 8, 128, mnT,
                         lambda b0, n, mt=mt: mnT.ap[:, b0:b0 + n, mt * 128:(mt + 1) * 128])
    for mt in range(2):
        for which, (Wm, od) in enumerate(((Wmk, o_mk), (Wmv, o_mv))):
            def consMK(g, c0, w, ps, ps_ap, mt=mt, which=which, od=od):
                f = mkf[which]
                evac(f.ap[:, :], ps_ap, [ps], [f])
                k.dma("pool", od[mt * 128:(mt + 1) * 128, :], f.ap[:, :], f, False)
                if which == 0:
                    evac(mkb.ap[:, mt, :], f.ap[:, :], [f], [mkb])
                else:
                    evac(mvb[NB].ap[:, mt, :], f.ap[:, :], [f], [mvb[NB]])
            dense(mnT, lambda c, mt=mt: mnT.ap[:, c, mt * 128:(mt + 1) * 128], KC, Wm,
                  lambda c, c0, w, Wm=Wm: Wm.ap[:, c, c0:c0 + w], 128, 256, consMK)
    make_mkT(NB)
    for b in range(NB):
        ms = mst[b % 2]
        k.dma("sp", ms.ap[:, 0:512].rearrange("p (mt c) -> p mt c", c=256),
              cmk[b * NMEM:(b + 1) * NMEM, :].rearrange("(mt p) c -> p mt c", p=128), ms, True)
        k.dma("sp", ms.ap[:, 512:1024].rearrange("p (mt c) -> p mt c", c=256),
              cmv[b * NMEM:(b + 1) * NMEM, :].rearrange("(mt p) c -> p mt c", p=128), ms, True)
        evac(mkb.ap[:, :, :], ms.ap[:, 0:512].rearrange("p (mt c) -> p mt c", c=256), [ms], [mkb])
        evac(mvb[b].ap[:, :, :], ms.ap[:, 512:1024].rearrange("p (mt c) -> p mt c", c=256), [ms], [mvb[b]])
        make_mkT(b)
    xq_b = sbX("xq_b", [128, XH * XD], BF16)
    xqT = sbX("xqT", [XD, XH, 128], BF16)
    xqTm = sbX("xqTm", [XD, XH, NB, PS], BF16)
    p_x = sbX("p_x", [128, XH * NMEM], BF16)
    pTx = sbX("pTx", [128, 8, 128], BF16)
    pTxm = sbX("pTxm", [128, 8, NB, PS], BF16)
    ox = sbX("ox", [128, XH * XD], BF16)
    oxT = sbX("oxT", [128, 2, 128], BF16)
    sgx = sbX("sgx", [128, D], F32)
    def _tileX(t, um):
        P = rows(t)
        smp = (t == TP)
        uTb = um.get(t)
        um.load(t + 1)

        def consXQ(g, c0, w, ps, ps_ap, P=P):
            evac(xq_b.ap[:P, :], ps_ap, [ps], [xq_b])
        dense(uTb, lambda c, uTb=uTb, P=P: uTb.ap[:, c, 0:P], KC, Wxq, lambda c, c0, w: Wxq.ap[:, c, c0:c0 + w],
              P, 256, consXQ)
        transpose_blocks(xq_b, lambda b, P=P: xq_b.ap[:P, b * XD:(b + 1) * XD], XH, P, xqT,
                         lambda b0, n, P=P: xqT.ap[:, b0:b0 + n, 0:P], bw=XD)
        um.prefetch(t + 1)
        if smp:
            for hh in range(XH):
                k.op("pool", lambda h, hh=hh: h.tensor_tensor(
                    out=xqTm.ap[:, hh, :, :], in0=bmx.ap[:XD, :].rearrange("p (b c) -> p b c", c=PS),
                    in1=xqT.ap[:, hh:hh + 1, 0:PS].to_broadcast([XD, NB, PS]), op=ALU.mult), [bmx, xqT], [xqTm])
        for hd in range(XH):
            ps = S[hd // 2]
            o_ap = ps.ap[:P, (hd % 2) * NMEM:(hd % 2 + 1) * NMEM]
            if not smp:
                k.ops("pe", [lambda h, o_ap=o_ap, hd=hd, P=P: h.matmul(
                    o_ap, xqT.ap[:, hd, 0:P], mkT[NB].ap[:, hd, :],
                    start=True, stop=True)], [xqT, mkT[NB]], [ps])
            else:
                fns = [lambda h, o_ap=o_ap, hd=hd, b=b: h.matmul(
                    o_ap, xqTm.ap[:, hd, b, :], mkT[b].ap[:, hd, :],
                    start=(b == 0), stop=(b == NB - 1)) for b in range(NB)]
                k.ops("pe", fns, [xqTm] + mkT[0:NB], [ps])
        sall = S_t[:P, 0:XH * NMEM].rearrange("p (h m) -> p h m", m=NMEM)
        k.op("dve", lambda h, sall=sall, P=P: h.tensor_reduce(out=smX.ap[:P, 0:4], in_=sall, axis=AX.X, op=ALU.max),
             [S[0], S[1]], [smX])
        k.op("dve", lambda h, P=P: h.tensor_scalar(out=smX.ap[:P, 4:8], in0=smX.ap[:P, 0:4], scalar1=-XSCALE,
                                                   scalar2=None, op0=ALU.mult), [smX], [smX])
        for hd in range(XH):
            ps = S[hd // 2]
            k.op("act", lambda h, hd=hd, ps=ps, P=P: h.activation(
                out=p_x.ap[:P, hd * NMEM:(hd + 1) * NMEM], in_=ps.ap[:P, (hd % 2) * NMEM:(hd % 2 + 1) * NMEM],
                func=AF.Exp, bias=smX.ap[:P, 4 + hd:5 + hd], scale=XSCALE, accum_out=smX.ap[:P, 10 + hd:11 + hd]),
                [ps, smX], [p_x, smX])
        k.op("dve", lambda h, P=P: h.reciprocal(out=smX.ap[:P, 0:4], in_=smX.ap[:P, 10:14]), [smX], [smX])
        transpose_blocks(p_x, lambda b, P=P: p_x.ap[:P, b * 128:(b + 1) * 128], 8, P, pTx,
                         lambda b0, n, P=P: pTx.ap[:, b0:b0 + n, 0:P])
        if smp:
            for j in range(8):
                en = ("pool", "dve")[j % 2]
                k.op(en, lambda h, j=j: h.tensor_tensor(
                    out=pTxm.ap[:, j, :, :], in0=bmx.ap[:, :].rearrange("p (b c) -> p b c", c=PS),
                    in1=pTx.ap[:, j:j + 1, 0:PS].to_broadcast([128, NB, PS]), op=ALU.mult), [bmx, pTx], [pTxm])
        for hd in range(XH):
            o_ap = O_b.ap[:P, hd * XD:(hd + 1) * XD]
            if not smp:
                fns = [lambda h, o_ap=o_ap, hd=hd, mc_=mc_, P=P: h.matmul(
                    o_ap, pTx.ap[:, hd * 2 + mc_, 0:P], mvb[NB].ap[:, mc_, hd * XD:(hd + 1) * XD],
                    start=(mc_ == 0), stop=(mc_ == 1)) for mc_ in range(2)]
                k.ops("pe", fns, [pTx, mvb[NB]], [O_b])
            else:
                fns = [lambda h, o_ap=o_ap, hd=hd, mc_=mc_, b=b: h.matmul(
                    o_ap, pTxm.ap[:, hd * 2 + mc_, b, :], mvb[b].ap[:, mc_, hd * XD:(hd + 1) * XD],
                    start=(b == 0 and mc_ == 0), stop=(b == NB - 1 and mc_ == 1))
                    for b in range(NB) for mc_ in range(2)]
                k.ops("pe", fns, [pTxm] + mvb[0:NB], [O_b])
        for hd in range(XH):
            k.op("act", lambda h, hd=hd, P=P: h.activation(out=ox.ap[:P, hd * XD:(hd + 1) * XD],
                                                          in_=O_b.ap[:P, hd * XD:(hd + 1) * XD], func=AF.Copy,
                                                          scale=smX.ap[:P, hd:hd + 1]), [O_b, smX], [ox])
        transpose_blocks(ox, lambda b, P=P: ox.ap[:P, b * 128:(b + 1) * 128], 2, P, oxT,
                         lambda b0, n, P=P: oxT.ap[:, b0:b0 + n, 0:P])
        gated_add(t, P, uTb, oxT, lambda c, P=P: oxT.ap[:, c, 0:P], 2, Wxo, WgC, sgx)
    for t in range(NT):
        _tileX(t, um)
    k.barrier()
    esX.close()

    esF = ExitStack()

    def sbF(name, shape, dt):
        return alloc(esF, name, shape, dt)

    Wout = sbF("Wout", [128, KC, D], BF16)
    Wfg = sbF("Wfg", [128, KC, DFF], BF16)
    Wfu = sbF("Wfu", [128, KC, DFF], BF16)
    xtF = [sbF("xtF%d" % i, [128, D], F32) for i in range(2)]
    f_b = sbF("f_b", [128, D], BF16)
    fT = sbF("fT", [128, KC, 128], BF16)
    act_b = sbF("act_b", [128, DFF], BF16)
    actT = sbF("actT", [128, FC, 128], BF16)
    sgt = sbF("sgt", [128, 512], F32)
    gpo = sbF("gpo", [128, D], F32)
    gfo = sbF("gfo", [128, D], F32)
    gfi = sbF("gfi", [128, D], F32)
    smF = sbF("smF", [128, 16], F32)
    wdr = [sbF("wdr%d" % i, [128, 2, D], BF16) for i in range(3)]
    scr = Buf(None, "wd_scr")
    k.dma("sp", xtF[0].ap[:, :], x_all[0:128, :], xtF[0], True)
    load_w(Wout, w_o, D, D)
    k.dma("sp", gpo.ap[:, :], g_post[0].partition_broadcast(128), gpo, True)
    k.dma("sp", gfo.ap[:, :], g_fpost[0].partition_broadcast(128), gfo, True)
    k.dma("sp", gfi.ap[:, :], g_fpre[0].partition_broadcast(128), gfi, True)
    for c in range(FC):
        k.dma("pool", wd_scr[c * 128:(c + 1) * 128, :], w_fd[c * 128:(c + 1) * 128, :], scr, True)
    load_w(Wfg, w_fg, D, DFF)
    load_w(Wfu, w_fu, D, DFF)
    wd_i = [0]

    def _tileF(t):
        P = rows(t)
        if t + 1 < NT:
            P1 = rows(t + 1)
            k.dma("sp", xtF[(t + 1) % 2].ap[:P1, :], x_all[tok0(t + 1):tok0(t + 1) + P1, :], xtF[(t + 1) % 2], True)
        xb = xtF[t % 2]
        transpose_blocks(mix[t], lambda b, t=t, P=P: mix[t].ap[:P, b * 128:(b + 1) * 128], 8, P, fT,
                         lambda b0, n, P=P: fT.ap[:, b0:b0 + n, 0:P])
        res = dense(fT, lambda c, P=P: fT.ap[:, c, 0:P], KC, Wout, lambda c, c0, w: Wout.ap[:, c, c0:c0 + w], P, D,
                    None, hold=True)
        rstd_of([r[4] for r in res], [r[3] for r in res], P, smF, D, col=0)
        for (g, c0, w, ps, ps_ap) in res:
            k.op("dve", lambda h, c0=c0, w=w, ps_ap=ps_ap, P=P: h.scalar_tensor_tensor(
                out=sgt.ap[:P, 0:w], in0=ps_ap, scalar=smF.ap[:P, 0:1], in1=gpo.ap[:P, c0:c0 + w],
                op0=ALU.mult, op1=ALU.mult), [ps, smF, gpo], [sgt])
            k.op("pool", lambda h, c0=c0, w=w, xb=xb, P=P: h.tensor_tensor(
                out=xb.ap[:P, c0:c0 + w], in0=xb.ap[:P, c0:c0 + w], in1=sgt.ap[:P, 0:w], op=ALU.add),
                [sgt, xb], [xb])
        rstd_of([xb.ap[:P, :]], [xb], P, smF, D, col=1)
        k.op("dve", lambda h, xb=xb, P=P: h.scalar_tensor_tensor(
            out=f_b.ap[:P, :], in0=xb.ap[:P, :], scalar=smF.ap[:P, 1:2], in1=gfi.ap[:P, :],
            op0=ALU.mult, op1=ALU.mult), [xb, smF, gfi], [f_b])
        transpose_blocks(f_b, lambda b, P=P: f_b.ap[:P, b * 128:(b + 1) * 128], 8, P, fT,
                         lambda b0, n, P=P: fT.ap[:, b0:b0 + n, 0:P])
        for c0 in range(0, DFF, 512):
            w = min(512, DFF - c0)
            pg_, pu_ = nextS(), nextS()
            for (ps, Wm) in ((pg_, Wfg), (pu_, Wfu)):
                fns = [lambda h, c=c, c0=c0, w=w, ps=ps, Wm=Wm, P=P: h.matmul(
                    ps.ap[:P, 0:w], fT.ap[:, c, 0:P], Wm.ap[:, c, c0:c0 + w], start=(c == 0), stop=(c == KC - 1))
                    for c in range(KC)]
                k.ops("pe", fns, [fT, Wm], [ps])
            k.op("act", lambda h, w=w, pg_=pg_, P=P: h.activation(out=sgt.ap[:P, 0:w], in_=pg_.ap[:P, 0:w],
                                                                 func=AF.Silu), [pg_], [sgt])
            k.op("dve", lambda h, c0=c0, w=w, pu_=pu_, P=P: h.tensor_tensor(
                out=act_b.ap[:P, c0:c0 + w], in0=pu_.ap[:P, 0:w], in1=sgt.ap[:P, 0:w], op=ALU.mult),
                [pu_, sgt], [act_b])
        transpose_blocks(act_b, lambda b, P=P: act_b.ap[:P, b * 128:(b + 1) * 128], FC, P, actT,
                         lambda b0, n, P=P: actT.ap[:, b0:b0 + n, 0:P])
        pa, pb_ = nextS(), nextS()
        for cc in range(FC // 2):
            wb = wdr[wd_i[0] % 3]
            wd_i[0] += 1
            k.dma("sp", wb.ap[:, :, :], wd_scr[cc * 256:(cc + 1) * 256, :].rearrange("(j p) n -> p j n", p=128), wb,
                  True, R=[scr])
            for (ps, n0) in ((pa, 0), (pb_, 512)):
                fns = [lambda h, j=j, cc=cc, ps=ps, n0=n0, wb=wb, P=P: h.matmul(
                    ps.ap[:P, 0:512], actT.ap[:, cc * 2 + j, 0:P], wb.ap[:, j, n0:n0 + 512],
                    start=(cc == 0 and j == 0), stop=(cc == FC // 2 - 1 and j == 1)) for j in range(2)]
                k.ops("pe", fns, [actT, wb], [ps])
        rstd_of([pa.ap[:P, 0:512], pb_.ap[:P, 0:512]], [pa, pb_], P, smF, D, col=2)
        for (ps, n0) in ((pa, 0), (pb_, 512)):
            k.op("dve", lambda h, ps=ps, n0=n0, P=P: h.scalar_tensor_tensor(
                out=sgt.ap[:P, 0:512], in0=ps.ap[:P, 0:512], scalar=smF.ap[:P, 2:3], in1=gfo.ap[:P, n0:n0 + 512],
                op0=ALU.mult, op1=ALU.mult), [ps, smF, gfo], [sgt])
            k.op("pool", lambda h, n0=n0, xb=xb, P=P: h.tensor_tensor(
                out=xb.ap[:P, n0:n0 + 512], in0=xb.ap[:P, n0:n0 + 512], in1=sgt.ap[:P, 0:512], op=ALU.add),
                [sgt, xb], [xb])
        k.dma("pool", y_all[tok0(t):tok0(t) + P, :], xb.ap[:P, :], xb, False)
    for t in range(NT):
        _tileF(t)
    k.finish()
    k.emit()
    esF.close()
    es.close()
    return nc


_CACHE = {}


def _consts(TP, NPG, NB):
    PS = NB * DEC_SEQ
    SEQ = TP * 128
    past = NPG * 128
    pos = np.concatenate([np.arange(SEQ, dtype=np.float32),
                          np.tile(past + np.arange(DEC_SEQ, dtype=np.float32), NB)]).astype(np.float32)

    def tables(half, nh):
        inv = (ROPE_BASE ** (-np.arange(half, dtype=np.float32) / half)).astype(np.float32)
        ang = (pos[:, None] * inv[None, :]).astype(np.float32)
        cos, sin = np.cos(ang).astype(np.float32), np.sin(ang).astype(np.float32)
        c2 = np.concatenate([cos, cos], axis=1)
        s2 = np.concatenate([-sin, sin], axis=1)
        return np.tile(c2, (1, nh)).astype(np.float32), np.tile(s2, (1, nh)).astype(np.float32)

    cosR, sinR = tables(64, RH)
    cosM, sinM = tables(32, MH)
    gam = np.array([1.0 - 2.0 ** (-5.0 - h) for h in range(RH)], dtype=np.float64)
    sc = DK ** -0.5
    i = np.arange(128)
    dtp = np.zeros((128, RH, 128), np.float64)
    qdp = np.zeros((128, RH, 128), np.float64)
    kdp = np.zeros((128, RH, DK), np.float64)
    dts = np.zeros((128, RH, 128), np.float64)
    qds = np.zeros((128, RH, 128), np.float64)
    kds = np.zeros((128, RH, DK), np.float64)
    for h in range(RH):
        diff = i[None, :] - i[:, None]
        dtp[:, h, :] = np.where(diff >= 0, gam[h] ** np.maximum(diff, 0), 0.0) * sc
        qdp[:, h, :] = (gam[h] ** (i + 1.0))[:, None]
        kdp[:, h, :] = (gam[h] ** (127.0 - i))[:, None] * sc
        ts = np.arange(PS)
        bj, jj = ts // DEC_SEQ, ts % DEC_SEQ
        same = bj[:, None] == bj[None, :]
        dl = jj[None, :] - jj[:, None]
        dts[:PS, h, :PS] = np.where(same & (dl >= 0), gam[h] ** np.maximum(dl, 0), 0.0) * sc
        qds[:PS, h, :] = (gam[h] ** (jj + 1.0))[:, None]
        kds[:PS, h, :] = (gam[h] ** (DEC_SEQ - 1.0 - jj))[:, None] * sc
    bm = np.zeros((128, NB, PS), np.float32)
    for b in range(NB):
        bm[:, b, b * DEC_SEQ:(b + 1) * DEC_SEQ] = 1.0
    bmr = np.zeros((128, NB), np.float32)
    for tkn in range(PS):
        bmr[tkn, tkn // DEC_SEQ] = 1.0
    caus = np.where(i[None, :] <= i[:, None], 1.0, 0.0).astype(np.float32).astype(ml_dtypes.bfloat16)
    big = np.full((128, 128), NEG, np.float32)
    for h in range(MH):
        for ii in range(DEC_SEQ):
            big[h * DEC_SEQ + ii, 64:64 + ii + 1] = 0.0
    f = lambda a: np.ascontiguousarray(a.reshape(a.shape[0], -1).astype(np.float32))
    goff = np.tile((np.arange(128) % 16).astype(np.int32)[:, None], (1, NB * (NPG // 8)))
    return dict(c_goff=goff, c_ident=np.eye(128, dtype=np.float32).astype(ml_dtypes.bfloat16),
                c_cosR=cosR, c_sinR=sinR, c_cosM=cosM, c_sinM=sinM,
                c_dtp=f(dtp), c_dts=f(dts), c_qdp=f(qdp), c_qds=f(qds), c_kdp=f(kdp), c_kds=f(kds),
                c_bm=f(bm).astype(ml_dtypes.bfloat16), c_bmr=bmr.astype(ml_dtypes.bfloat16), c_caus=caus, c_big=big)


def kernel(x_prompt, x_sample, mem_prompt, cache_ckv, cache_kpe, page_table, state_ret,
           cache_mem_k, cache_mem_v, norm_mix_pre, norm_mix_post, norm_ffn_pre, norm_ffn_post,
           norm_mem, norm_q_lat, norm_kv_lat, w_in, w_uq, w_uk, w_uv, w_mem_k, w_mem_v,
           w_ret_o, w_mla_o, w_x_o, w_out, w_ffn_gate, w_ffn_up, w_ffn_down, _prepare_only=False):
    A = lambda a: np.ascontiguousarray(np.asarray(a))
    x_prompt, x_sample, mem_prompt = A(x_prompt), A(x_sample), A(mem_prompt)
    B, SEQ, _ = x_prompt.shape
    DB = x_sample.shape[0]
    NC = B
    NB = DB // NC
    TP = SEQ // 128
    NPOOL = cache_ckv.shape[1]
    NPG = page_table.shape[1]
    PS = NB * DEC_SEQ
    key = (TP, NPG, NPOOL, NB)
    if key not in _CACHE:
        _CACHE[key] = build(*key)
    nc = _CACHE[key]
    consts = _consts(TP, NPG, NB)
    ckv = A(cache_ckv)[0]
    kpe = A(cache_kpe)[0]
    col = lambda a: A(a)[0].reshape(-1, 1)
    row = lambda a: A(a)[0].reshape(1, -1)
    shared = dict(
        c_ckv=ckv, c_kpe=kpe,
        g_pre=row(norm_mix_pre), g_post=row(norm_mix_post), g_fpre=row(norm_ffn_pre), g_fpost=row(norm_ffn_post),
        g_mem=row(norm_mem), g_ql=row(norm_q_lat), g_kv=row(norm_kv_lat),
        w_in=A(w_in)[0], w_uq=A(w_uq)[0], w_uk=A(w_uk)[0].reshape(MH * KVL, DN),
        w_uv=A(w_uv)[0].reshape(MH * KVL, DVH), w_mk=A(w_mem_k)[0], w_mv=A(w_mem_v)[0], w_ro=A(w_ret_o)[0],
        w_mo=A(w_mla_o)[0], w_xo=A(w_x_o)[0], w_o=A(w_out)[0], w_fg=A(w_ffn_gate)[0], w_fu=A(w_ffn_up)[0],
        w_fd=A(w_ffn_down)[0], **consts)
    pt = A(page_table).astype(np.int32)
    st = A(state_ret)[0]
    mk = A(cache_mem_k)[0]
    mv = A(cache_mem_v)[0]
    in_maps = []
    for c in range(NC):
        sl = slice(c * NB, (c + 1) * NB)
        m = dict(shared)
        m["x_all"] = np.concatenate([x_prompt[c], x_sample[sl].reshape(PS, D)], axis=0)
        m["mem_p"] = mem_prompt[c]
        m["ptab"] = pt[sl].reshape(1, NB * NPG)
        m["st_in"] = st[sl].reshape(NB * RH * DK, DV)
        m["cmk"] = mk[sl].reshape(NB * NMEM, XH * XD)
        m["cmv"] = mv[sl].reshape(NB * NMEM, XH * XD)
        in_maps.append(m)
    if _prepare_only:
        return nc, in_maps, (NC, NB, SEQ)
    res = run_bass_kernel_spmd(nc, in_maps, core_ids=list(range(NC))).results
    return _assemble(res, NC, NB, SEQ)


def _assemble(res, NC, NB, SEQ):
    f32 = np.float32
    y_p = np.stack([res[c]["y_all"][:SEQ] for c in range(NC)]).astype(f32)
    y_s = np.concatenate([res[c]["y_all"][SEQ:].reshape(NB, DEC_SEQ, D) for c in range(NC)]).astype(f32)
    ckv_p = np.stack([res[c]["o_ckv"][:SEQ] for c in range(NC)])[None].astype(f32)
    kpe_p = np.stack([res[c]["o_kpe"][:SEQ] for c in range(NC)])[None].astype(f32)
    ckv_s = np.concatenate([res[c]["o_ckv"][SEQ:].reshape(NB, DEC_SEQ, KVL) for c in range(NC)])[None].astype(f32)
    kpe_s = np.concatenate([res[c]["o_kpe"][SEQ:].reshape(NB, DEC_SEQ, DR) for c in range(NC)])[None].astype(f32)
    ret_p = np.stack([res[c]["o_retp"].reshape(RH, DK, DV) for c in range(NC)])[None].astype(f32)
    ret_s = np.concatenate([res[c]["o_rets"].reshape(NB, RH, DK, DV) for c in range(NC)])[None].astype(f32)
    mk_p = np.stack([res[c]["o_mk"].reshape(NMEM, XH, XD) for c in range(NC)])[None].astype(f32)
    mv_p = np.stack([res[c]["o_mv"].reshape(NMEM, XH, XD) for c in range(NC)])[None].astype(f32)
    return (y_p, y_s, ckv_p, kpe_p, ckv_s, kpe_s, ret_p, ret_s, mk_p, mv_p)
```
